# Optimizing a Trainium2 kernel written in Bass

```python
import math
import jax, jax.numpy as jnp
from jax import lax
import numpy as np

D_MODEL = 1024
BATCH = 8
SEQ = 2048
DEPTH = 1
DEC_BATCH = 128
DEC_SEQ = 8
PAST_LEN = 16384
PAGE_SIZE = 128

D_MIX = 2 * D_MODEL
HG_WIDTH = D_MIX // 2
HG_DK = 128
HG_DV = 128
HG_HEADS = HG_WIDTH // HG_DV
HG_CHUNK = 16
M_WIDTH = D_MIX - HG_WIDTH
M_HEADDIM = 64
M_HEADS = M_WIDTH // M_HEADDIM
M_DSTATE = 128
M_GROUPS = 2
M_CONV = 4
M_CHUNK = 128
M_CONV_DIM = M_WIDTH + 2 * M_GROUPS * M_DSTATE
D_FF = 4 * D_MODEL
NORM_EPS = 1e-5
SPLITS = (HG_HEADS * HG_DK, HG_HEADS * HG_DK, HG_WIDTH, HG_WIDTH, M_WIDTH, M_CONV_DIM, M_HEADS)
D_IN_PROJ = sum(SPLITS)
SPLIT_POINTS = tuple(sum(SPLITS[:i + 1]) for i in range(len(SPLITS) - 1))

kernel_name = 'hymba_hgrn2_ssd_decoder_step'


def rmsnorm(x, gain):
    xf = x.astype(jnp.float32)
    xf = xf * lax.rsqrt(jnp.mean(xf * xf, axis=-1, keepdims=True) + NORM_EPS)
    return (xf * gain.astype(jnp.float32)).astype(x.dtype)


def chunk_len(L, target):
    return L if L <= target else math.gcd(L, target)


def hgrn2_recurrence(q, k, v, log_f, s0):
    bsz, L, H, K = q.shape
    C = chunk_len(L, HG_CHUNK)
    n = L // C

    def blk(t):
        return t.reshape(bsz, n, C, H, t.shape[-1]).transpose(1, 0, 2, 3, 4)

    q, k, v, a = blk(q), blk(k), blk(v), blk(log_f)
    cum = jnp.cumsum(a, axis=2)
    cum_last = cum[:, :, -1:]
    qd = q * jnp.exp(cum)
    kd = k * jnp.exp(-cum)
    k_end = k * jnp.exp(cum_last - cum)
    chunk_decay = jnp.exp(cum_last[:, :, 0])
    causal = jnp.tril(jnp.ones((C, C), bool))
    att = jnp.einsum('cbthk,cbshk->cbhts', qd, kd)
    att = jnp.where(causal, att, 0.0)
    o_intra = jnp.einsum('cbhts,cbshv->cbthv', att, v)

    def step(s, inp):
        qd_c, ke_c, v_c, dec_c = inp
        o = jnp.einsum('bthk,bhkv->bthv', qd_c, s)
        s = dec_c[..., None] * s + jnp.einsum('bshk,bshv->bhkv', ke_c, v_c)
        return s, o

    s_fin, o_inter = lax.scan(step, s0, (qd, k_end, v, chunk_decay))
    o = (o_intra + o_inter).transpose(1, 0, 2, 3, 4).reshape(bsz, L, H, v.shape[-1])
    return o, s_fin


def ssd_scan(xs, dt, a, bm, cm, s0):
    bsz, L, H, P = xs.shape
    G, N = bm.shape[2], bm.shape[3]
    R = H // G
    C = chunk_len(L, M_CHUNK)
    n = L // C
    log_a = (dt * a).reshape(bsz, n, C, G, R).transpose(1, 0, 2, 3, 4)
    xdt = (xs * dt[..., None]).reshape(bsz, n, C, G, R, P).transpose(1, 0, 2, 3, 4, 5)
    bm = bm.reshape(bsz, n, C, G, N).transpose(1, 0, 2, 3, 4)
    cm = cm.reshape(bsz, n, C, G, N).transpose(1, 0, 2, 3, 4)
    cum = jnp.cumsum(log_a, axis=2)
    seg = cum[:, :, :, None] - cum[:, :, None, :]
    causal = jnp.tril(jnp.ones((C, C), bool))[:, :, None, None]
    decay = jnp.exp(jnp.where(causal, seg, -jnp.inf))
    cb = jnp.einsum('cbtgn,cbsgn->cbtsg', cm, bm)
    y_intra = jnp.einsum('cbtsg,cbtsgr,cbsgrp->cbtgrp', cb, decay, xdt)
    cum_last = cum[:, :, -1]
    to_end = jnp.exp(cum_last[:, :, None] - cum)
    from_start = jnp.exp(cum)
    chunk_decay = jnp.exp(cum_last)

    def step(s, inp):
        c_c, b_c, xdt_c, fs_c, te_c, cd_c = inp
        y = jnp.einsum('btgn,bgrpn->btgrp', c_c, s) * fs_c[..., None]
        s = cd_c[..., None, None] * s + jnp.einsum('bsgn,bsgr,bsgrp->bgrpn', b_c, te_c, xdt_c)
        return s, y

    s_fin, y_inter = lax.scan(step, s0.reshape(bsz, G, R, P, N),
                              (cm, bm, xdt, from_start, to_end, chunk_decay))
    y = (y_intra + y_inter).transpose(1, 0, 2, 3, 4, 5).reshape(bsz, L, H, P)
    return y, s_fin.reshape(bsz, H, P, N)


def decoder_layer(x, hg_s0, ssm_s0, conv0, lb, ln1, w_in, hg_norm, conv_w, conv_b,
                  dt_bias, a_log, d_skip, m_norm, w_out, ln2, w_up, w_down):
    f32 = jnp.float32
    bsz, L, _ = x.shape
    h = rmsnorm(x, ln1)
    proj = jnp.einsum('bld,de->ble', h, w_in)
    q, f_raw, i_in, g, z, xbc, dt_raw = jnp.split(proj, SPLIT_POINTS, axis=-1)

    f = lb + (1.0 - lb) * jax.nn.sigmoid(f_raw.astype(f32))
    log_f = jnp.log(f)
    k = 1.0 - f
    o_hg, hg_s = hgrn2_recurrence(
        q.astype(f32).reshape(bsz, L, HG_HEADS, HG_DK),
        k.reshape(bsz, L, HG_HEADS, HG_DK),
        i_in.astype(f32).reshape(bsz, L, HG_HEADS, HG_DV),
        log_f.reshape(bsz, L, HG_HEADS, HG_DK),
        hg_s0.astype(f32))
    o_hg = rmsnorm(o_hg, hg_norm).reshape(bsz, L, HG_WIDTH) * jax.nn.silu(g.astype(f32))

    xbc_full = jnp.concatenate([conv0.astype(xbc.dtype), xbc], axis=1)
    conv_new = xbc_full[:, L:]
    acc = conv_b.astype(f32)
    for j in range(M_CONV):
        acc = acc + xbc_full[:, j:j + L].astype(f32) * conv_w[j].astype(f32)
    xbc_act = jax.nn.silu(acc)
    xs, bm, cm = jnp.split(xbc_act, [M_WIDTH, M_WIDTH + M_GROUPS * M_DSTATE], axis=-1)
    xs = xs.reshape(bsz, L, M_HEADS, M_HEADDIM)
    bm = bm.reshape(bsz, L, M_GROUPS, M_DSTATE)
    cm = cm.reshape(bsz, L, M_GROUPS, M_DSTATE)
    dt = jax.nn.softplus(dt_raw.astype(f32) + dt_bias.astype(f32))
    a = -jnp.exp(a_log.astype(f32))
    y, ssm_s = ssd_scan(xs, dt, a, bm, cm, ssm_s0.astype(f32))
    y = y + d_skip.astype(f32)[:, None] * xs
    y = y.reshape(bsz, L, M_WIDTH) * jax.nn.silu(z.astype(f32))
    y = rmsnorm(y.reshape(bsz, L, M_GROUPS, M_WIDTH // M_GROUPS),
                m_norm.reshape(M_GROUPS, M_WIDTH // M_GROUPS)).reshape(bsz, L, M_WIDTH)

    mix = jnp.concatenate([o_hg, y], axis=-1).astype(x.dtype)
    x = x + jnp.einsum('ble,ed->bld', mix, w_out)

    u = jax.nn.relu(jnp.einsum('bld,df->blf', rmsnorm(x, ln2), w_up))
    x = x + jnp.einsum('blf,fd->bld', u * u, w_down)
    return x, hg_s.astype(x.dtype), ssm_s.astype(x.dtype), conv_new


def setup_inputs(seed: int = 0) -> dict:
    key = jax.random.key(seed)
    ks = jax.random.split(key, 24)
    nrm = jax.random.normal
    dt0 = jnp.exp(jax.random.uniform(ks[10], (DEPTH, M_HEADS), minval=math.log(1e-3), maxval=math.log(1e-1)))
    return {
        'x_prompt': nrm(ks[0], (BATCH, SEQ, D_MODEL), jnp.float32),
        'x_sample': nrm(ks[1], (DEC_BATCH, DEC_SEQ, D_MODEL), jnp.float32),
        'state_hgrn': 0.5 * nrm(ks[2], (DEPTH, DEC_BATCH, HG_HEADS, HG_DK, HG_DV), jnp.float32),
        'state_ssm': 0.3 * nrm(ks[3], (DEPTH, DEC_BATCH, M_HEADS, M_HEADDIM, M_DSTATE), jnp.float32),
        'state_conv': nrm(ks[4], (DEPTH, DEC_BATCH, M_CONV - 1, M_CONV_DIM), jnp.float32),
        'hg_lb_logits': 0.1 * nrm(ks[5], (DEPTH + 1, HG_HEADS * HG_DK), jnp.float32),
        'ln1': 1.0 + 0.02 * nrm(ks[6], (DEPTH, D_MODEL), jnp.float32),
        'w_in': nrm(ks[7], (DEPTH, D_MODEL, D_IN_PROJ), jnp.float32) * D_MODEL ** -0.5,
        'hg_norm': 1.0 + 0.02 * nrm(ks[8], (DEPTH, HG_HEADS, HG_DV), jnp.float32),
        'conv_w': nrm(ks[9], (DEPTH, M_CONV, M_CONV_DIM), jnp.float32) * M_CONV ** -0.5,
        'conv_b': 0.02 * nrm(ks[11], (DEPTH, M_CONV_DIM), jnp.float32),
        'dt_bias': dt0 + jnp.log(-jnp.expm1(-dt0)),
        'a_log': jnp.log(jax.random.uniform(ks[12], (DEPTH, M_HEADS), minval=1.0, maxval=16.0)),
        'd_skip': 1.0 + 0.1 * nrm(ks[13], (DEPTH, M_HEADS), jnp.float32),
        'm_norm': 1.0 + 0.02 * nrm(ks[14], (DEPTH, M_WIDTH), jnp.float32),
        'w_out': nrm(ks[15], (DEPTH, D_MIX, D_MODEL), jnp.float32) * D_MIX ** -0.5,
        'ln2': 1.0 + 0.02 * nrm(ks[16], (DEPTH, D_MODEL), jnp.float32),
        'w_up': nrm(ks[17], (DEPTH, D_MODEL, D_FF), jnp.float32) * D_MODEL ** -0.5,
        'w_down': nrm(ks[18], (DEPTH, D_FF, D_MODEL), jnp.float32) * D_FF ** -0.5,
        'ln_f': 1.0 + 0.02 * nrm(ks[19], (D_MODEL,), jnp.float32),
    }


def reference(x_prompt, x_sample, state_hgrn, state_ssm, state_conv, hg_lb_logits, ln1, w_in,
              hg_norm, conv_w, conv_b, dt_bias, a_log, d_skip, m_norm, w_out, ln2, w_up, w_down, ln_f):
    lb_all = jnp.cumsum(jax.nn.softmax(hg_lb_logits.astype(jnp.float32), axis=0), axis=0)
    yp, ys = x_prompt, x_sample
    bp = x_prompt.shape[0]
    hgp, hgs, ssp, sss, cvp, cvs = [], [], [], [], [], []
    for l in range(DEPTH):
        w = (ln1[l], w_in[l], hg_norm[l], conv_w[l], conv_b[l], dt_bias[l], a_log[l],
             d_skip[l], m_norm[l], w_out[l], ln2[l], w_up[l], w_down[l])
        hg0 = jnp.zeros((bp, HG_HEADS, HG_DK, HG_DV), yp.dtype)
        ssm0 = jnp.zeros((bp, M_HEADS, M_HEADDIM, M_DSTATE), yp.dtype)
        conv0 = jnp.zeros((bp, M_CONV - 1, M_CONV_DIM), yp.dtype)
        yp, h_p, s_p, c_p = decoder_layer(yp, hg0, ssm0, conv0, lb_all[l], *w)
        ys, h_s, s_s, c_s = decoder_layer(ys, state_hgrn[l], state_ssm[l], state_conv[l], lb_all[l], *w)
        hgp.append(h_p); hgs.append(h_s)
        ssp.append(s_p); sss.append(s_s)
        cvp.append(c_p); cvs.append(c_s)
    y_prompt = rmsnorm(yp, ln_f)
    y_sample = rmsnorm(ys, ln_f)
    return (y_prompt, y_sample, jnp.stack(hgp), jnp.stack(hgs), jnp.stack(ssp), jnp.stack(sss),
            jnp.stack(cvp), jnp.stack(cvs))
```

```python
import numpy as np
from contextlib import ExitStack
import concourse.bass as bass
import concourse.mybir as mybir
from concourse.bass_utils import run_bass_kernel_spmd

F32 = mybir.dt.float32
BF = mybir.dt.bfloat16
AF = mybir.ActivationFunctionType
ALU = mybir.AluOpType

NCORES = 8
D = 1024
L = 2048
NTP = 16
NT = 17
DIN = 6672
EPS = 1e-5
ENG = ('pe', 'act', 'dve', 'pool', 'sp')


class Sch:
    def __init__(self, nc, es, ndsem=(('sp', 14), ('pool', 6), ('act', 4))):
        self.nc = nc
        self.prog = {e: [] for e in ENG}
        self.cnt = {e: 0 for e in ENG}
        self.sem = {e: es.enter_context(nc.semaphore('prog_' + e)) for e in ENG}
        self.waited = {e: {} for e in ENG}
        self.lastw = {}
        self.readers = {}
        self.dsem = {}
        self.duse = {}
        self.dnext = {}
        for q, n in ndsem:
            self.dsem[q] = [es.enter_context(nc.semaphore('d_%s_%d' % (q, i))) for i in range(n)]
            self.duse[q] = [0] * n
            self.dnext[q] = 0

    def _need(self, eng, ev):
        kind, key, val = ev
        if kind == 'e' and key == eng and eng in ('pe', 'sp'):
            return
        k = (kind, key)
        if val > self.waited[eng].get(k, 0):
            self.prog[eng].append(('wait', k, val))
            self.waited[eng][k] = val

    def _deps(self, eng, r, w):
        for res in r:
            if res in self.lastw:
                self._need(eng, self.lastw[res])
            if res[0] == 'P':
                for k, val in self.readers.get(res, {}).items():
                    if k != ('e', eng):
                        self._need(eng, (k[0], k[1], val))
        for res in w:
            if res in self.lastw:
                self._need(eng, self.lastw[res])
            for k, val in self.readers.get(res, {}).items():
                self._need(eng, (k[0], k[1], val))

    def _record(self, ev, r, w):
        k = (ev[0], ev[1])
        for res in r:
            d = self.readers.setdefault(res, {})
            if ev[2] > d.get(k, 0):
                d[k] = ev[2]
        for res in w:
            self.lastw[res] = ev
            self.readers[res] = {}

    def op(self, eng, fn, r=(), w=(), inc=True):
        self._deps(eng, r, w)
        n = self.cnt[eng] + 1
        if inc:
            self.cnt[eng] = n
        import sys as _s
        self.prog[eng].append(('op', fn, inc, _s._getframe(1).f_lineno, (tuple(r), tuple(w))))
        self._record(('e', eng, n), r, w)
        return eng

    def dma(self, q, out, in_, r=(), w=(), **kw):
        i = self.dnext[q]
        self.dnext[q] = (i + 1) % len(self.dsem[q])
        uses = self.duse[q][i]
        if uses > 0:
            self._need(q, ('d', (q, i), 16 * uses))
        self._deps(q, r, w)
        self.duse[q][i] = uses + 1
        self.prog[q].append(('dma', out, in_, (q, i), kw))
        self._record(('d', (q, i), 16 * (uses + 1)), r, w)

    def barrier(self):
        for e in ENG:
            for e2 in ENG:
                if e2 != e and self.cnt[e2] > 0:
                    self._need(e, ('e', e2, self.cnt[e2]))
            for q in self.dsem:
                for i, u in enumerate(self.duse[q]):
                    if u > 0:
                        self._need(e, ('d', (q, i), 16 * u))

    def finish(self):
        for q in self.dsem:
            for i, u in enumerate(self.duse[q]):
                if u > 0:
                    self._need('sp', ('d', (q, i), 16 * u))
        for e2 in ENG:
            if e2 != 'sp' and self.cnt[e2] > 0:
                self._need('sp', ('e', e2, self.cnt[e2]))

    def _semof(self, k):
        return self.sem[k[1]] if k[0] == 'e' else self.dsem[k[1][0]][k[1][1]]

    def check(self):
        pc = {e: 0 for e in ENG}
        val = {}
        progress = True
        while progress:
            progress = False
            for e in ENG:
                while pc[e] < len(self.prog[e]):
                    it = self.prog[e][pc[e]]
                    if it[0] == 'wait':
                        if val.get(it[1], 0) >= it[2]:
                            pc[e] += 1; progress = True
                        else:
                            break
                    elif it[0] == 'op':
                        if it[2]:
                            val[('e', e)] = val.get(('e', e), 0) + 1
                        pc[e] += 1; progress = True
                    else:
                        k = ('d', it[3])
                        val[k] = val.get(k, 0) + 16
                        pc[e] += 1; progress = True
        stuck = {e: (pc[e], len(self.prog[e]), self.prog[e][pc[e]][:3] if pc[e] < len(self.prog[e]) else None,
                     [it[3:] for it in self.prog[e][pc[e]:pc[e] + 3] if it[0] == 'op']) for e in ENG}
        ok = all(pc[e] == len(self.prog[e]) for e in ENG)
        if not ok:
            print('DEADLOCK CHECK STUCK', stuck)
        return ok

    def emit(self):
        nc = self.nc
        if not self.check():
            raise RuntimeError('semaphore schedule would deadlock')
        with nc.Block() as block:
            for e, reg in (('sp', block.sync), ('pe', block.tensor), ('act', block.scalar),
                           ('dve', block.vector), ('pool', block.gpsimd)):
                def body(eo, e=e):
                    for it in self.prog[e]:
                        if it[0] == 'wait':
                            eo.wait_ge(self._semof(it[1]), it[2])
                        elif it[0] == 'op':
                            ins = it[1](eo)
                            if it[2]:
                                ins.then_inc(self.sem[e], 1)
                        else:
                            eo.dma_start(out=it[1], in_=it[2], **it[4]).then_inc(
                                self.dsem[it[3][0]][it[3][1]], 16)
                reg(body)


class Arena:
    def __init__(self, t, nwords):
        self.t = t
        self.n = nwords
        self.off = 0

    def f32(self, n):
        assert self.off + n <= self.n, ('arena overflow', self.off, n, self.n)
        v = self.t[:, self.off:self.off + n]
        self.off += n
        return v

    def bf(self, n):
        w = (n + 1) // 2
        return self.f32(w).bitcast(BF)


def mm(out, lhsT, rhs, start=True, stop=True):
    return lambda pe: pe.matmul(out, lhsT, rhs, start=start, stop=stop)


def tr(out, in_, ident):
    return lambda pe: pe.transpose(out, in_, ident)


def actf(out, in_, func, **kw):
    return lambda a: a.activation(out=out, in_=in_, func=func, **kw)


def tt(out, a, b, op):
    return lambda v: v.tensor_tensor(out=out, in0=a, in1=b, op=op)


def ts(out, a, s1, s2, op0, op1=ALU.bypass):
    if s2 is None:
        return lambda v: v.tensor_scalar(out=out, in0=a, scalar1=s1, scalar2=None, op0=op0)
    return lambda v: v.tensor_scalar(out=out, in0=a, scalar1=s1, scalar2=s2, op0=op0, op1=op1)


def stt(out, in0, scalar, in1, op0, op1):
    return lambda v: v.scalar_tensor_tensor(out=out, in0=in0, scalar=scalar, in1=in1, op0=op0, op1=op1)


def cp(out, in_):
    return lambda v: v.tensor_copy(out=out, in_=in_)


def v3(ap, h=8):
    return ap.rearrange('p (h t) -> p h t', h=h)


C_IDENT, C_TRIP, C_TRIS, C_STRP, C_STRS, C_ONES, C_ROWM = 0, 128, 256, 384, 512, 640, 768
C_TOT = 784


def make_consts():
    c = np.zeros((128, C_TOT), np.float32)
    s = np.arange(128)[:, None]
    t = np.arange(128)[None, :]
    same = (s // 8) == (t // 8)
    c[:, C_IDENT:C_IDENT + 128] = (s == t)
    c[:, C_TRIP:C_TRIP + 128] = (s <= t)
    c[:, C_TRIS:C_TRIS + 128] = (s <= t) & same
    c[:, C_STRP:C_STRP + 128] = (s > t)
    c[:, C_STRS:C_STRS + 128] = (s > t) & same
    c[:, C_ONES:C_ONES + 128] = 1.0
    c[:, C_ROWM:C_ROWM + 16] = (s // 8) == np.arange(16)[None, :]
    return c


STAG_H = 0
PE_RUN_MAX = 7
STAG_M = 0


def build(stop_after=None, dbg=False):
    nc = bass.Bass('TRN2', target_bir_lowering=False)
    es = ExitStack()

    def din(name, shape):
        return nc.dram_tensor(name, list(shape), F32, kind='ExternalInput')

    def dout(name, shape):
        return nc.dram_tensor(name, list(shape), F32, kind='ExternalOutput')

    xp_h = din('xp', (L, D)); xs_h = din('xs', (128, D))
    sth_h = din('st_h', (16, 8, 128, 128)); sts_h = din('st_s', (16, 1024, 128)); stc_h = din('st_c', (48, 1536))
    lbl_h = din('lbl', (2, 1024)); ln1_h = din('ln1', (1024,))
    wh_h = din('w_h', (2, 128, 16384)); woh_h = din('wo_h', (2, 128, 4096))
    wm_h = din('w_m', (2, 128, 8 * 1296)); wom_h = din('wo_m', (2, 128, 4096))
    wu_h = din('w_u', (8, 128, 4096)); wd_h = din('w_d', (8, 128, 4096))
    hgn_h = din('hgn', (1024,)); mcst_h = din('mcst', (128, 100))
    mn_h = din('m_norm', (1024,)); ln2_h = din('ln2', (1024,)); lnf_h = din('ln_f', (1024,))
    cst_h = din('cst', (128, C_TOT))
    yp_h = dout('yp', (L, D)); ys_h = dout('ys', (128, D))
    hgp_h = dout('hgp', (8, 128, 128)); hgs_h = dout('hgs', (16, 8, 128, 128))
    ssp_h = dout('ssp', (1024, 128)); sss_h = dout('sss', (16, 1024, 128))
    cvp_h = dout('cvp', (3, 1536)); cvs_h = dout('cvs', (48, 1536))
    xp, xs, sth, sts, stc = xp_h.ap(), xs_h.ap(), sth_h.ap(), sts_h.ap(), stc_h.ap()
    yp, ys, hgp, hgs, ssp, sss, cvp, cvs = (yp_h.ap(), ys_h.ap(), hgp_h.ap(), hgs_h.ap(),
                                             ssp_h.ap(), sss_h.ap(), cvp_h.ap(), cvs_h.ap())
    dbg_h = dout('dbg', (128, NT, 2048)) if dbg else None

    AW = 52800
    arena_t = es.enter_context(nc.sbuf_tensor('arena', [128, AW], F32))
    ar = Arena(arena_t, AW)
    ps_t = es.enter_context(nc.psum_tensor('psum', [128, 4096], F32))
    P1 = ps_t[:, 0:1024]; P2 = ps_t[:, 1024:2048]; P3 = ps_t[:, 2048:3072]
    PT = ps_t[:, 3072:3584].bitcast(BF)
    P4 = ps_t[:, 3584:4096]
    S = Sch(nc, es)

    def col_ap(h, n_c):
        return bass.AP(h, 0, [[1, 128], [128, n_c]])

    def bc_ap(h, n, off=0):
        return bass.AP(h, off, [[0, 128], [1, n]])

    cst = ar.f32(C_TOT)
    x1 = ar.f32(NT * D)
    hTall = ar.bf(8 * NT * 128); hTv_all = hTall.rearrange('p (c t) -> p c t', c=8)
    identb = ar.bf(128); onesb = ar.bf(128)
    g1col = ar.f32(8); hgncol = ar.f32(8); mncol = ar.f32(8); g2col = ar.f32(8)
    rstd_s = ar.f32(8)
    epsc = ar.f32(8)
    mcp = ar.f32(100)
    S.dma('sp', cst, cst_h.ap(), w=['cst'])
    S.dma('sp', g1col, col_ap(ln1_h, 8), w=['g1col'], allow_slow_non_contiguous=True)
    S.dma('sp', hgncol, col_ap(hgn_h, 8), w=['hgncol'], allow_slow_non_contiguous=True)
    S.dma('sp', mncol, col_ap(mn_h, 8), w=['mncol'], allow_slow_non_contiguous=True)
    S.dma('sp', g2col, col_ap(ln2_h, 8), w=['g2col'], allow_slow_non_contiguous=True)
    S.dma('sp', mcp, mcst_h.ap(), w=['mcp'])
    identf = cst[:, C_IDENT:C_IDENT + 128]
    onesf = cst[:, C_ONES:C_ONES + 128]
    rowm = cst[:, C_ROWM:C_ROWM + 16]
    S.op('dve', cp(identb, identf), r=['cst'], w=['identb'])
    S.op('dve', cp(onesb, onesf), r=['cst'], w=['onesb'])
    S.op('pool', lambda g: g.memset(epsc, EPS), w=['epsc'])
    persist_mark = ar.off

    held = set()
    bstate = [0]

    def bank(hold=True):
        for _ in range(16):
            i = bstate[0] % 8
            bstate[0] += 1
            if i not in held:
                if hold:
                    held.add(i)
                return ps_t[:, i * 512:(i + 1) * 512], 'PS%d' % i
        raise RuntimeError('no free PSUM bank')

    def release(name):
        held.discard(int(name[2:]))

    def run_chains(chain, K=2, stagger=0):
        active = []
        free = list(range(K))
        nxt = 0
        since = 10 ** 9
        while active or nxt < NT:
            while len(active) < K and nxt < NT and (not active or since >= stagger):
                T_ = ORDER[nxt]
                PAR[T_] = free.pop(0)
                active.append((chain(T_), T_))
                nxt += 1
                since = 0
            since += 1
            for it_ in list(active):
                try:
                    n_ = 0
                    while next(it_[0]) == 'pe' and n_ < PE_RUN_MAX:
                        n_ += 1
                except StopIteration:
                    active.remove(it_)
                    free.append(PAR[it_[1]])

    ORDER = list(range(NT))
    PAR = {}

    def rms_to_hT(xt, xtag, xn, xnres, hTv, hres, gcol, gres, junk, jres, on_act=False):
        if on_act:
            S.op('act', actf(junk, xt, AF.Square, accum_out=rstd_s[:, 0:1]), r=[xtag], w=[jres, 'ssq'])
        else:
            S.op('dve', (lambda v, xt=xt, junk=junk: v.scalar_tensor_tensor(
                out=junk, in0=xt, scalar=1.0, in1=xt, op0=ALU.mult, op1=ALU.mult, accum_out=rstd_s[:, 0:1])),
                r=[xtag], w=[jres, 'ssq'])
        S.op('act', actf(rstd_s[:, 1:2], rstd_s[:, 0:1], AF.Ln, scale=1.0 / D, bias=epsc[:, 0:1]),
             r=['ssq', 'epsc'], w=['lnv'])
        S.op('act', actf(rstd_s[:, 2:3], rstd_s[:, 1:2], AF.Exp, scale=-0.5), r=['lnv'], w=['rstd'])
        S.op('act', actf(xn, xt, AF.Copy, scale=rstd_s[:, 2:3]), r=[xtag, 'rstd'], w=[xnres])
        bk, bn = bank()
        bkb = bk.bitcast(BF)
        for c in range(8):
            S.op('pe', tr(bkb[:, c * 128:(c + 1) * 128], xn[:, c * 128:(c + 1) * 128], identb),
                 r=[xnres, 'identb'], w=[bn], inc=(c == 7))
        for c in range(8):
            S.op('dve', ts(hTv[:, c, :], bkb[:, c * 128:(c + 1) * 128], gcol[:, c:c + 1], None, ALU.mult),
                 r=[bn, gres], w=[hres])
        release(bn)

    def xsrc(T):
        return xp[T * 128:(T + 1) * 128, :] if T < NTP else xs

    def x1t(T):
        return x1[:, T * D:(T + 1) * D]

    for T in ORDER[:3]:
        S.dma('sp', x1t(T), xsrc(T), w=['x1_%d' % T])

    import itertools
    uid = itertools.count()
    prefetched = set()

    def wview(off, nbf):
        return arena_t[:, persist_mark + off:persist_mark + off + nbf // 2].bitcast(BF)

    def prefetch_after(phase):
        allH = ['WH%d' % k for k in range(8)]
        allM = ['WM%d' % k for k in range(8)]
        if phase == 'H0':
            for c in range(8):
                S.dma('pool', wview(c * 1024, 2048), wh_h.ap()[1][:, c * 2048:(c + 1) * 2048], w=['WH%d' % c])
            prefetched.add('H1')
        elif phase == 'H1':
            for c in range(8):
                S.dma('pool', wview(c * 648, 1296), wm_h.ap()[0][:, c * 1296:(c + 1) * 1296], w=allH + ['WM%d' % c])
            S.dma('pool', wview(5184, 4096), wom_h.ap()[0], w=allH + ['WoM'])
            prefetched.add('M0'); prefetched.add('WoM0')
        elif phase == 'M0':
            for c in range(8):
                S.dma('pool', wview(c * 648, 1296), wm_h.ap()[1][:, c * 1296:(c + 1) * 1296], w=['WM%d' % c])
            prefetched.add('M1')
        elif phase == 'M1':
            for s_ in range(2):
                S.dma('pool', wview(s_ * 2048, 4096), wu_h.ap()[s_], w=allM + ['Wu%d' % s_])
            prefetched.add('Wu')

    def phase_H(hb):
        S.barrier()
        ar.off = persist_mark
        WH = ar.bf(8 * 2048); WHv = WH.rearrange('p (c k e) -> p c k e', c=8, k=4)
        WoH = ar.bf(4 * 1024); WoHv = WoH.rearrange('p (c e) -> p c e', c=4)
        if ('H%d' % hb) not in prefetched:
            for c in range(8):
                S.dma('pool', WH[:, c * 2048:(c + 1) * 2048], wh_h.ap()[hb][:, c * 2048:(c + 1) * 2048], w=['WH%d' % c])
        S.dma('pool', WoH, woh_h.ap()[hb], w=['WoH'])
        oml = ar.f32(512); lbt = ar.f32(1024)
        S.dma('sp', lbt.rearrange('p (a n) -> p a n', a=2), bass.AP(lbl_h, hb * 512, [[0, 128], [1024, 2], [1, 512]]),
              w=['S0f0', 'S0f1'])
        S.op('dve', tt(oml, lbt[:, 0:512], lbt[:, 512:1024], ALU.subtract), r=['S0f0', 'S0f1'], w=['oml'])
        S.op('act', actf(oml, oml, AF.Exp), r=['oml'], w=['oml'])
        S.op('act', actf(oml, oml, AF.Ln, bias=1.0), r=['oml'], w=['oml'])
        S.op('act', actf(oml, oml, AF.Exp, scale=-1.0), r=['oml'], w=['oml'])
        f2 = lambda: [ar.f32(512), ar.f32(512)]
        b2 = lambda: [ar.bf(512), ar.bf(512)]
        kf, logf, sgg = f2(), f2(), f2()
        vtok, ktok = b2(), b2()
        eo, Epos, er = f2(), f2(), f2()
        kend, qdT, kdT, attT, osq, mixT = b2(), b2(), b2(), b2(), b2(), b2()
        St = [ar.f32(512), ar.f32(512)]; Sbf = [ar.bf(512) for _ in range(3)]
        S0f = [lbt[:, 0:512], lbt[:, 512:1024]]; S0b = b2(); kendb = b2(); oc = ar.f32(512)
        print('H arena words', ar.off, 'of', ar.n)
        S.op('pool', lambda g: g.memset(St[0], 0.0), w=['St0'])
        S.op('pool', lambda g: g.memset(St[1], 0.0), w=['St1'])
        S.op('pool', lambda g: g.memset(Sbf[0], 0.0), w=['Sbf0'])
        qbank = {}
        hsl = [slice(h * 128, (h + 1) * 128) for h in range(4)]

        def front(T):
            p = PAR[T]
            q = T % 2
            hv = hTv_all[:, :, T * 128:(T + 1) * 128]
            hres = 'hT_%d' % T
            if hb == 0:
                if T + 3 < NT:
                    yield S.dma('sp', x1t(T + 3), xsrc(T + 3), w=['x1_%d' % (T + 3)])
                rms_to_hT(x1t(T), 'x1_%d' % T, er[p].bitcast(BF), 'er%d' % p, hv, hres, g1col, 'g1col',
                          eo[p].bitcast(BF), 'eo%d' % p, on_act=True)
                yield None
            bA, nA = bank()
            for c in range(8):
                yield S.op('pe', mm(bA, hv[:, c, :], WHv[:, c, 1, :], start=(c == 0), stop=(c == 7)),
                     r=[hres, 'WH%d' % c], w=[nA], inc=(c == 7))
            bB, nB = bank()
            for c in range(8):
                yield S.op('pe', mm(bB, hv[:, c, :], WHv[:, c, 2, :], start=(c == 0), stop=(c == 7)),
                     r=[hres, 'WH%d' % c], w=[nB], inc=(c == 7))
            yield S.op('act', actf(kf[p], bA, AF.Exp), r=[nA], w=['kf%d' % p])
            release(nA)
            yield S.op('act', actf(kf[p], kf[p], AF.Ln, bias=1.0), r=['kf%d' % p], w=['kf%d' % p])
            yield S.op('act', actf(kf[p], kf[p], AF.Exp, scale=-1.0), r=['kf%d' % p], w=['kf%d' % p])
            yield S.op('dve', tt(kf[p], kf[p], oml, ALU.mult), r=['kf%d' % p, 'oml'], w=['kf%d' % p])
            yield S.op('act', actf(logf[p], kf[p], AF.Ln, scale=-1.0, bias=1.0), r=['kf%d' % p], w=['logf%d' % p])
            yield S.op('dve', cp(ktok[p], kf[p]), r=['kf%d' % p], w=['ktok%d' % p])
            yield S.op('act', actf(vtok[p], bB, AF.Copy), r=[nB], w=['vtok%d' % p])
            release(nB)
            bC, nC = bank(hold=True)
            for h in range(4):
                for c in range(8):
                    yield S.op('pe', mm(bC[:, hsl[h]], WHv[:, c, 0, hsl[h]], hv[:, c, :], start=(c == 0), stop=(c == 7)),
                         r=[hres, 'WH%d' % c], w=[nC], inc=(c == 7 and h == 3))
            qbank[T] = (bC, nC)
            bD, nD = bank()
            for h in range(4):
                for c in range(8):
                    yield S.op('pe', mm(bD[:, hsl[h]], WHv[:, c, 3, hsl[h]], hv[:, c, :], start=(c == 0), stop=(c == 7)),
                         r=[hres, 'WH%d' % c], w=[nD], inc=(c == 7 and h == 3))
            yield S.op('act', actf(sgg[p], bD, AF.Exp, scale=-1.0), r=[nD], w=['sgg%d' % p])
            yield S.op('act', actf(sgg[p], sgg[p], AF.Ln, bias=1.0), r=['sgg%d' % p], w=['sgg%d' % p])
            yield S.op('act', actf(sgg[p], sgg[p], AF.Exp, scale=-1.0), r=['sgg%d' % p], w=['sgg%d' % p])
            yield S.op('dve', tt(sgg[p], bD, sgg[p], ALU.mult), r=[nD, 'sgg%d' % p], w=['sgg%d' % p])
            release(nD)

        def back(T):
            p = PAR[T]
            q = T % 2
            samp = (T == NTP)
            c0 = C_TRIS if samp else C_TRIP
            tri = cst[:, c0:c0 + 128]
            c0 = C_STRS if samp else C_STRP
            strict = cst[:, c0:c0 + 128]
            R = lambda n: '%s%d' % (n, p)
            bE, nE = bank()
            yield S.op('pe', mm(bE, strict, logf[p]), r=['cst', R('logf')], w=[nE])
            yield S.op('act', actf(eo[p], bE, AF.Exp), r=[nE], w=[R('eo')])
            release(nE)
            yield S.op('dve', tt(kend[p], ktok[p], eo[p], ALU.mult), r=[R('ktok'), R('eo')], w=[R('kend')])
            bF, nF = bank()
            for h in range(4):
                yield S.op('pe', mm(bF[:, hsl[h]], logf[p][:, hsl[h]], tri), r=['cst', R('logf')], w=[nF], inc=(h == 3))
            yield S.op('act', actf(Epos[p], bF, AF.Exp), r=[nF], w=[R('Epos')])
            yield S.op('act', actf(er[p], bF, AF.Exp, scale=-1.0), r=[nF], w=[R('er')])
            release(nF)
            if not samp:
                bM, nM = bank()
                for h in range(4):
                    yield S.op('pe', mm(bM[:, hsl[h]], kend[p][:, hsl[h]], vtok[p][:, hsl[h]]), r=[R('kend'), R('vtok')],
                         w=[nM], inc=(h == 3))
                for h in range(4):
                    yield S.op('dve', stt(St[1 - q][:, hsl[h]], St[q][:, hsl[h]], Epos[p][:, h * 128 + 127:h * 128 + 128],
                                          bM[:, hsl[h]], ALU.mult, ALU.add), r=['St%d' % q, R('Epos'), nM], w=['St%d' % (1 - q)])
                release(nM)
                yield S.op('act', actf(Sbf[(T + 1) % 3], St[1 - q], AF.Copy), r=['St%d' % (1 - q)], w=['Sbf%d' % ((T + 1) % 3)])
                if T == NTP - 1:
                    yield S.dma('sp', hgp[hb * 4:(hb + 1) * 4].rearrange('h k v -> k h v'), v3(St[1 - q], 4), r=['St%d' % (1 - q)],
                                w=['hgp_out%d' % next(uid)])
            bC, nC = qbank.pop(T)
            yield S.op('dve', tt(qdT[p], bC, Epos[p], ALU.mult), r=[nC, R('Epos')], w=[R('qdT')])
            release(nC)
            bG, nG = bank()
            bGb = bG.bitcast(BF)
            for h in range(4):
                yield S.op('pe', tr(bGb[:, hsl[h]], ktok[p][:, hsl[h]], identb), r=[R('ktok'), 'identb'], w=[nG], inc=(h == 3))
            yield S.op('dve', tt(kdT[p], bGb[:, 0:512], er[p], ALU.mult), r=[nG, R('er')], w=[R('kdT')])
            release(nG)
            bH, nH = bank()
            for h in range(4):
                yield S.op('pe', mm(bH[:, hsl[h]], kdT[p][:, hsl[h]], qdT[p][:, hsl[h]]), r=[R('kdT'), R('qdT')], w=[nH],
                     inc=(h == 3))
            yield S.op('dve', tt(v3(attT[p], 4), v3(bH, 4), tri.unsqueeze(1).to_broadcast([128, 4, 128]), ALU.mult),
                 r=[nH, 'cst'], w=[R('attT')])
            release(nH)
            if not samp:
                bI, nI = bank()
                for h in range(4):
                    yield S.op('pe', mm(bI[:, hsl[h]], vtok[p][:, hsl[h]], attT[p][:, hsl[h]], start=True, stop=False),
                         r=[R('vtok'), R('attT')], w=[nI], inc=False)
                    yield S.op('pe', mm(bI[:, hsl[h]], Sbf[T % 3][:, hsl[h]], qdT[p][:, hsl[h]], start=False, stop=True),
                         r=['Sbf%d' % (T % 3), R('qdT')], w=[nI], inc=(h == 3))
                yield S.op('act', actf(eo[p], bI, AF.Copy), r=[nI], w=[R('eo')])
                release(nI)
            else:
                bP, nP = bank(hold=True)
                yield S.dma('sp', v3(S0f[0], 4), sth[0][hb * 4:(hb + 1) * 4].rearrange('h k v -> k h v'), w=['S0f0'])
                for b in range(16):
                    bs = b % 2
                    if b + 1 < 16:
                        yield S.dma('sp', v3(S0f[1 - bs], 4), sth[b + 1][hb * 4:(hb + 1) * 4].rearrange('h k v -> k h v'),
                              w=['S0f%d' % (1 - bs)])
                    yield S.op('act', actf(S0b[bs], S0f[bs], AF.Copy), r=['S0f%d' % bs], w=['S0b%d' % bs])
                    for h in range(4):
                        yield S.op('pe', mm(bP[:, h * 128 + 8 * b:h * 128 + 8 * b + 8], S0b[bs][:, hsl[h]],
                                      qdT[p][:, h * 128 + 8 * b:h * 128 + 8 * b + 8]),
                             r=['S0b%d' % bs, R('qdT')], w=[nP], inc=(h == 3))
                    yield S.op('dve', ts(kendb[bs], kend[p], rowm[:, b:b + 1], None, ALU.mult), r=[R('kend'), 'cst'],
                         w=['kendb%d' % bs])
                    bQ, nQ = bank()
                    for h in range(4):
                        yield S.op('pe', mm(bQ[:, hsl[h]], kendb[bs][:, hsl[h]], vtok[p][:, hsl[h]]),
                             r=['kendb%d' % bs, R('vtok')], w=[nQ], inc=(h == 3))
                    for h in range(4):
                        yield S.op('dve', stt(S0f[bs][:, hsl[h]], S0f[bs][:, hsl[h]],
                                        Epos[p][:, h * 128 + 8 * b + 7:h * 128 + 8 * b + 8], bQ[:, hsl[h]], ALU.mult, ALU.add),
                             r=['S0f%d' % bs, R('Epos'), nQ], w=['S0f%d' % bs])
                    release(nQ)
                    yield S.dma('sp', hgs[b][hb * 4:(hb + 1) * 4].rearrange('h k v -> k h v'), v3(S0f[bs], 4),
                          r=['S0f%d' % bs], w=['hgs_out%d' % next(uid)])
                bI, nI = bank()
                for h in range(4):
                    yield S.op('pe', mm(bI[:, hsl[h]], vtok[p][:, hsl[h]], attT[p][:, hsl[h]]), r=[R('vtok'), R('attT')], w=[nI],
                         inc=(h == 3))
                yield S.op('act', actf(oc, bP, AF.Copy), r=[nP], w=['oc'])
                release(nP)
                yield S.op('dve', tt(eo[p], bI, oc, ALU.add), r=[nI, 'oc'], w=[R('eo')])
                release(nI)
            osb = eo[p]
            yield S.op('act', actf(osq[p], osb, AF.Square), r=[R('eo')], w=[R('osq')])
            bJ, nJ = bank()
            yield S.op('pe', mm(bJ, onesb, osq[p]), r=['onesb', R('osq')], w=[nJ])
            yield S.op('act', actf(er[p], bJ, AF.Ln, scale=1.0 / 128, bias=epsc[:, 0:1]), r=[nJ, 'epsc'], w=[R('er')])
            release(nJ)
            yield S.op('act', actf(er[p], er[p], AF.Exp, scale=-0.5), r=[R('er')], w=[R('er')])
            yield S.op('dve', tt(osb, osb, er[p], ALU.mult), r=[R('eo'), R('er')], w=[R('eo')])
            for h in range(4):
                yield S.op('dve', stt(mixT[p][:, hsl[h]], osb[:, hsl[h]], hgncol[:, hb * 4 + h:hb * 4 + h + 1], sgg[p][:, hsl[h]],
                                ALU.mult, ALU.mult), r=[R('eo'), R('sgg'), 'hgncol'], w=[R('mixT')])
            mixv = v3(mixT[p], 4)
            for n in range(2):
                bK, nK = bank()
                for c in range(4):
                    yield S.op('pe', mm(bK, mixv[:, c, :], WoHv[:, c, n * 512:(n + 1) * 512], start=(c == 0), stop=(c == 3)),
                         r=[R('mixT'), 'WoH'], w=[nK], inc=(c == 3))
                xs_ = x1t(T)[:, n * 512:(n + 1) * 512]
                yield S.op('dve', tt(xs_, bK, xs_, ALU.add), r=[nK, 'x1_%d' % T], w=['x1_%d' % T])
                release(nK)

        def chain(T):
            yield from front(T)
            if T == ORDER[-1]:
                prefetch_after('H%d' % hb)
            yield from back(T)

        run_chains(chain, stagger=STAG_H)

    def dump_x1():
        for T in range(NT):
            dst = yp[T * 128:(T + 1) * 128, :] if T < NTP else ys
            S.dma('sp', dst, x1t(T), r=['x1_%d' % T], w=['y_out%d' % next(uid)])
        S.finish()
        S.emit()
        return nc

    phase_H(0)
    if stop_after == 'H0':
        return dump_x1()
    phase_H(1)
    if stop_after == 'H1':
        return dump_x1()

    def phase_M(g):
        S.barrier()
        ar.off = persist_mark
        WM = ar.bf(8 * 1296); WMv = WM.rearrange('p (c e) -> p c e', c=8)
        WoM = ar.bf(4 * 1024); WoMv = WoM.rearrange('p (c e) -> p c e', c=4)
        pieces = [(0, 4096 + g * 512, 512), (512, 5120 + g * 512, 512), (1024, 6144 + g * 128, 128),
                  (1152, 6400 + g * 128, 128), (1280, 6656, 16)]
        if ('M%d' % g) not in prefetched:
            for c in range(8):
                S.dma('pool', WM[:, c * 1296:(c + 1) * 1296], wm_h.ap()[g][:, c * 1296:(c + 1) * 1296], w=['WM%d' % c])
        if ('WoM%d' % g) not in prefetched:
            S.dma('pool', WoM, wom_h.ap()[g], w=['WoM'])
        mo = g * 50
        cwv = mcp[:, mo:mo + 24].rearrange('p (c j) -> p c j', c=6)
        cbt = mcp[:, mo + 24:mo + 30]; dcol = mcp[:, mo + 30:mo + 34]; dtb = mcp[:, mo + 34:mo + 42]
        abc = ar.f32(8)
        choff = [g * 512 + j * 128 for j in range(4)] + [1024 + g * 128, 1280 + g * 128]
        S.op('act', actf(abc, mcp[:, mo + 42:mo + 50], AF.Exp), r=['mcp'], w=['abc'])
        S.op('dve', ts(abc, abc, -1.0, None, ALU.mult), r=['abc'], w=['abc'])
        f2 = lambda n: [ar.f32(n), ar.f32(n)]
        b2 = lambda n: [ar.bf(n), ar.bf(n)]
        XB = f2(6 * 176)
        Xp = [x[:, 0:6 * 131].rearrange('p (c t) -> p c t', c=6) for x in XB]
        Xs = [x.rearrange('p (c b t) -> p c b t', c=6, b=16) for x in XB]
        acc = f2(768); xab = b2(768); sz = f2(512)
        tails = [ar.f32(18).rearrange('p (c t) -> p c t', c=6) for _ in range(3)]
        sm = f2(64)
        one = lambda x: [x, x]
        fsT = f2(512); Lh = f2(1024); MT = b2(1024); cbm = f2(128)
        xdtpad = b2(1024); xdtte = b2(512); Btok = b2(128)
        tmp = f2(512); ysq = b2(512); rstdy = f2(128); mix2 = b2(512)
        STt = [ar.f32(512), ar.f32(512)]; STb = [ar.bf(512) for _ in range(3)]
        S0n_all = ar.f32(1024); S0n = [S0n_all[:, 0:512], S0n_all[:, 512:1024]]; cvo = S0n_all[:, 0:768]
        S0T = one(ar.bf(512)); xdtb = one(ar.bf(512)); stcb = ar.f32(768)
        print('M arena words', ar.off, 'of', ar.n)
        S.op('pool', lambda g_: g_.memset(STt[0], 0.0), w=['STt0'])
        S.op('pool', lambda g_: g_.memset(STt[1], 0.0), w=['STt1'])
        S.op('pool', lambda g_: g_.memset(STb[0], 0.0), w=['STb0'])
        for p_ in range(2):
            S.op('pool', lambda g_, p_=p_: g_.memset(xdtpad[p_], 0.0), w=['xdtpad%d' % p_])
            S.op('pool', lambda g_, p_=p_: g_.memset(XB[p_], 0.0), w=['XB%d' % p_])
        jsl = [slice(j * 128, (j + 1) * 128) for j in range(8)]

        def front(T):
            p = PAR[T]
            q = T % 2
            samp = (T == NTP)
            R = lambda n: '%s%d' % (n, p)
            hv = hTv_all[:, :, T * 128:(T + 1) * 128]
            hres = 'hT_%d' % T
            bA, nA = bank()
            for j in range(4):
                for c in range(8):
                    yield S.op('pe', mm(bA[:, jsl[j]], WMv[:, c, j * 128:(j + 1) * 128], hv[:, c, :], start=(c == 0), stop=(c == 7)),
                         r=[hres, 'WM%d' % c], w=[nA], inc=(c == 7 and j == 3))
            yield S.op('act', actf(sz[p], bA, AF.Silu), r=[nA], w=[R('sz')])
            release(nA)
            bB, nB = bank()
            for j in range(4):
                for c in range(8):
                    yield S.op('pe', mm(bB[:, jsl[j]], WMv[:, c, 512 + j * 128:512 + (j + 1) * 128], hv[:, c, :],
                                  start=(c == 0), stop=(c == 7)), r=[hres, 'WM%d' % c], w=[nB], inc=(c == 7 and j == 3))
            bC, nC = bank()
            for j in range(2):
                for c in range(8):
                    yield S.op('pe', mm(bC[:, jsl[j]], WMv[:, c, 1024 + j * 128:1024 + (j + 1) * 128], hv[:, c, :],
                                  start=(c == 0), stop=(c == 7)), r=[hres, 'WM%d' % c], w=[nC], inc=False)
            for c in range(8):
                yield S.op('pe', mm(bC[:, 256:264], hv[:, c, :], WMv[:, c, 1280 + 8 * g:1288 + 8 * g], start=(c == 0), stop=(c == 7)),
                     r=[hres, 'WM%d' % c], w=[nC], inc=(c == 7))
            smp = sm[p]
            xdtr = smp[:, 0:8]; dtt = smp[:, 8:16]; la = smp[:, 16:24]
            yield S.op('dve', tt(xdtr, bC[:, 256:264], dtb, ALU.add), r=[nC, 'mcp'], w=[R('sm_x')])
            if not samp:
                yield S.op('act', actf(Xp[p][:, 0:4, 3:131], v3(bB, 4), AF.Copy), r=[nB], w=[R('XB')])
                yield S.op('act', actf(Xp[p][:, 4:6, 3:131], v3(bC[:, 0:256], 2), AF.Copy), r=[nC], w=[R('XB')])
                release(nB); release(nC)
                yield S.op('dve', cp(tails[T % 3], Xp[p][:, :, 128:131]), r=[R('XB')], w=['tail%d' % (T % 3)])
                if T > 0:
                    yield S.op('dve', cp(Xp[p][:, :, 0:3], tails[(T - 1) % 3]), r=['tail%d' % ((T - 1) % 3)], w=[R('XB')])
                if T == NTP - 1:
                    for k in range(3):
                        yield S.dma('sp', bass.AP(cvp_h, k * 1536 + g * 512, [[1, 128], [128, 4]]), Xp[p][:, 0:4, 128 + k],
                              r=[R('XB')], w=['cvp_out%d_%d' % (g, next(uid))], allow_slow_non_contiguous=True)
                        for bi in (4, 5):
                            yield S.dma('sp', bass.AP(cvp_h, k * 1536 + choff[bi], [[1, 128], [1, 1]]),
                                  Xp[p][:, bi, 128 + k:129 + k], r=[R('XB')], w=['cvp_out%d_%d' % (g, next(uid))])
            else:
                yield S.op('pool', lambda g_: g_.memset(XB[p], 0.0), r=[R('XB')], w=[R('XB')])
                for bi, (co, cn) in enumerate([(g * 512, 512), (1024 + g * 128, 128), (1280 + g * 128, 128)]):
                    o_ = 0 if bi == 0 else (512 if bi == 1 else 640)
                    yield S.dma('sp', stcb[0:48, o_:o_ + cn], stc[:, co:co + cn], w=['stcb'])
                bS, nS = bank()
                for j in range(6):
                    yield S.op('pe', tr(bS[:, j * 48:(j + 1) * 48], stcb[0:48, jsl[j]], identf[0:48, 0:48]), r=['stcb', 'cst'],
                         w=[nS], inc=(j == 5))
                yield S.op('dve', cp(Xs[p][:, :, :, 0:3], bS[:, 0:288].rearrange('p (c b t) -> p c b t', c=6, b=16)),
                     r=[nS], w=[R('XB')])
                release(nS)
                yield S.op('act', actf(Xs[p][:, 0:4, :, 3:11], bB.rearrange('p (c b t) -> p c b t', c=4, b=16), AF.Copy),
                     r=[nB], w=[R('XB')])
                yield S.op('act', actf(Xs[p][:, 4:6, :, 3:11], bC[:, 0:256].rearrange('p (c b t) -> p c b t', c=2, b=16), AF.Copy),
                     r=[nC], w=[R('XB')])
                release(nB); release(nC)
                yield S.op('dve', cp(cvo[:, 0:288].rearrange('p (c b t) -> p c b t', c=6, b=16), Xs[p][:, :, :, 8:11]),
                     r=[R('XB')], w=['S0n0', 'S0n1'])
                bV, nV = bank()
                bV2, nV2 = bank()
                for j in range(6):
                    dst = bV[0:48, jsl[j]] if j < 4 else bV2[0:48, jsl[j - 4]]
                    yield S.op('pe', tr(dst, cvo[:, j * 48:(j + 1) * 48], identf), r=['S0n0', 'S0n1', 'cst'], w=[nV if j < 4 else nV2],
                         inc=(j in (3, 5)))
                yield S.op('act', actf(stcb[0:48, 0:512], bV[0:48, :], AF.Copy), r=[nV], w=['stcb'])
                yield S.op('act', actf(stcb[0:48, 512:768], bV2[0:48, 0:256], AF.Copy), r=[nV2], w=['stcb'])
                release(nV); release(nV2)
                for bi, (co, cn) in enumerate([(g * 512, 512), (1024 + g * 128, 128), (1280 + g * 128, 128)]):
                    o_ = 0 if bi == 0 else (512 if bi == 1 else 640)
                    yield S.dma('sp', cvs[:, co:co + cn], stcb[0:48, o_:o_ + cn], r=['stcb'], w=['cvs_out%d_%d' % (g, next(uid))])
            accv = acc[p].rearrange('p (c t) -> p c t', c=6)
            accs = acc[p].rearrange('p (c b t) -> p c b t', c=6, b=16)
            for j in range(6):
                if not samp:
                    o_ = accv[:, j, :]; xin = lambda k, j=j: Xp[p][:, j, k:k + 128]
                else:
                    o_ = accs[:, j, :, :]; xin = lambda k, j=j: Xs[p][:, j, :, k:k + 8]
                an = 'acc%d_%d' % (p, j)
                yield S.op('act', actf(o_, xin(0), AF.Identity, scale=cwv[:, j, 0:1], bias=cbt[:, j:j + 1]),
                     r=[R('XB'), 'mcp'], w=[an])
            for k in range(1, 4):
                for j in range(6):
                    if not samp:
                        o_ = accv[:, j, :]; xin = lambda k, j=j: Xp[p][:, j, k:k + 128]
                    else:
                        o_ = accs[:, j, :, :]; xin = lambda k, j=j: Xs[p][:, j, :, k:k + 8]
                    an = 'acc%d_%d' % (p, j)
                    yield S.op('dve', stt(o_, xin(k), cwv[:, j, k:k + 1], o_, ALU.mult, ALU.add), r=[R('XB'), 'mcp', an], w=[an])
            accn = ['acc%d_%d' % (p, j) for j in range(6)]
            yield S.op('act', actf(acc[p], acc[p], AF.Silu), r=accn, w=accn)
            yield S.op('act', actf(xab[p], acc[p], AF.Copy), r=accn, w=[R('xab')])
            yield S.op('act', actf(xdtr, xdtr, AF.Exp), r=[R('sm_x')], w=[R('sm_x')])
            yield S.op('act', actf(dtt, xdtr, AF.Ln, bias=1.0), r=[R('sm_x')], w=[R('sm_dt')])
            yield S.op('dve', tt(la, dtt, abc, ALU.mult), r=[R('sm_dt'), 'abc'], w=[R('sm_la')])

        def back(T):
            p = PAR[T]
            q = T % 2
            samp = (T == NTP)
            R = lambda n: '%s%d' % (n, p)
            c0 = C_TRIS if samp else C_TRIP
            tri = cst[:, c0:c0 + 128]
            c0 = C_STRS if samp else C_STRP
            strict = cst[:, c0:c0 + 128]
            smp = sm[p]
            dtt = smp[:, 8:16]; la = smp[:, 16:24]; cumt = smp[:, 24:32]; te = smp[:, 32:40]
            dtte = smp[:, 40:48]; cd = smp[:, 48:56]
            xabv = xab[p].rearrange('p (c t) -> p c t', c=6)
            accn = ['acc%d_%d' % (p, j) for j in range(6)]
            bE, nE = bank()
            yield S.op('pe', mm(bE[:, 0:8], tri, la), r=['cst', R('sm_la')], w=[nE], inc=False)
            yield S.op('pe', mm(bE[:, 8:16], strict, la), r=['cst', R('sm_la')], w=[nE], inc=False)
            yield S.op('pe', mm(bE[:, 16:24], onesf, la), r=['cst', R('sm_la')], w=[nE])
            yield S.op('act', actf(cumt, bE[:, 0:8], AF.Copy, scale=-1.0), r=[nE], w=[R('sm_cum')])
            yield S.op('act', actf(te, bE[:, 8:16], AF.Exp), r=[nE], w=[R('sm_te')])
            yield S.op('act', actf(cd, bE[:, 16:24], AF.Exp), r=[nE], w=[R('sm_cd')])
            release(nE)
            yield S.op('dve', tt(dtte, dtt, te, ALU.mult), r=[R('sm_dt'), R('sm_te')], w=[R('sm_dtte')])
            bG, nG = bank()
            bGb = bG.bitcast(BF)
            for j in range(4):
                yield S.op('pe', tr(bGb[:, jsl[j]], xabv[:, j, :], identb), r=[R('xab'), 'identb'], w=[nG], inc=False)
            yield S.op('pe', tr(bGb[:, 512:640], xabv[:, 4, :], identb), r=[R('xab'), 'identb'], w=[nG])
            bH, nH = bank()
            yield S.op('pe', mm(bH[:, 0:128], xabv[:, 4, :], xabv[:, 5, :]), r=[R('xab')], w=[nH])
            yield S.op('act', actf(Btok[p], bGb[:, 512:640], AF.Copy), r=[nG], w=[R('Btok')])
            yield S.op('dve', tt(cbm[p], bH[:, 0:128], tri, ALU.mult), r=[nH, 'cst'], w=[R('cbm')])
            release(nH)
            xdtpv = xdtpad[p].rearrange('p (j e c) -> p j e c', j=4, e=2)
            PTv = bGb[:, 0:512].rearrange('p (j e c) -> p j e c', j=4, e=2)
            dtv = dtt.rearrange('p (j e) -> p j e', e=2)
            for e in range(2):
                yield S.op('dve', tt(xdtpv[:, :, e, e * 64:(e + 1) * 64], PTv[:, :, e, :],
                               dtv[:, :, e:e + 1].to_broadcast([128, 4, 64]), ALU.mult), r=[nG, R('sm_dt')], w=[R('xdtpad')])
            yield S.op('dve', tt(xdtte[p].rearrange('p (h c) -> p h c', h=8), bGb[:, 0:512].rearrange('p (h c) -> p h c', h=8),
                           dtte.unsqueeze(2).to_broadcast([128, 8, 64]), ALU.mult), r=[nG, R('sm_dtte')], w=[R('xdtte')])
            release(nG)
            if not samp:
                bM, nM = bank()
                yield S.op('pe', mm(bM, Btok[p], xdtte[p]), r=[R('Btok'), R('xdtte')], w=[nM])
                STo = STt[q].rearrange('p (h c) -> p h c', h=8)
                STn = STt[1 - q]
                yield S.op('dve', tt(STn.rearrange('p (h c) -> p h c', h=8), STo, cd.unsqueeze(2).to_broadcast([128, 8, 64]), ALU.mult),
                           r=['STt%d' % q, R('sm_cd')], w=['STt%d' % (1 - q)])
                yield S.op('dve', tt(STn, STn, bM, ALU.add), r=['STt%d' % (1 - q), nM], w=['STt%d' % (1 - q)])
                release(nM)
                yield S.op('act', actf(STb[(T + 1) % 3], STn, AF.Copy), r=['STt%d' % (1 - q)], w=['STb%d' % ((T + 1) % 3)])
                if T == NTP - 1:
                    bW, nW = bank()
                    for j in range(4):
                        yield S.op('pe', tr(bW[:, jsl[j]], STn[:, jsl[j]], identf), r=['STt%d' % (1 - q), 'cst'], w=[nW], inc=(j == 3))
                    yield S.op('act', actf(Lh[p][:, 0:512], bW, AF.Copy), r=[nW], w=[R('Lh')])
                    release(nW)
                    yield S.dma('sp', ssp[g * 512:(g + 1) * 512].rearrange('(j p) n -> p j n', p=128), v3(Lh[p][:, 0:512], 4),
                          r=[R('Lh')], w=['ssp_out%d' % next(uid)])
            for half in range(2):
                bI, nI = bank()
                for hh in range(4):
                    h = half * 4 + hh
                    yield S.op('pe', mm(bI[:, jsl[hh]], la[:, h:h + 1].to_broadcast([128, 128]), tri), r=[R('sm_la'), 'cst'],
                               w=[nI], inc=(hh == 3))
                for hh in range(4):
                    h = half * 4 + hh
                    yield S.op('act', actf(Lh[p][:, jsl[h]], bI[:, jsl[hh]], AF.Exp, bias=cumt[:, h:h + 1]),
                               r=[nI, R('sm_cum')], w=([R('Lh'), 'Lhh%d_0' % p] if h == 0 else ['Lhh%d_%d' % (p, h)]))
                for e in range(2):
                    yield S.op('act', actf(fsT[p][e * 64:(e + 1) * 64, half * 256:(half + 1) * 256].rearrange('p (a t) -> p a t', a=2),
                                           bI[e * 64:(e + 1) * 64, :].rearrange('p (a e t) -> p a e t', a=2, e=2)[:, :, e, :],
                                           AF.Exp), r=[nI], w=[R('fsT')])
                release(nI)
            Lhn = ['Lhh%d_%d' % (p, h) for h in range(8)]
            yield S.op('dve', stt(v3(MT[p]), v3(Lh[p]), 1.0, cbm[p].unsqueeze(1).to_broadcast([128, 8, 128]), ALU.min, ALU.mult),
                       r=Lhn + [R('Lh'), R('cbm')], w=[R('MT')])
            bJ, nJ = bank()
            if not samp:
                for j in range(4):
                    yield S.op('pe', mm(bJ[:, jsl[j]], STb[T % 3][:, jsl[j]], xabv[:, 5, :]), r=['STb%d' % (T % 3), R('xab')], w=[nJ],
                         inc=(j == 3))
            else:
                yield S.dma('sp', v3(S0n[0], 4), sts[0][g * 512:(g + 1) * 512].rearrange('(j p) n -> p j n', p=128), w=['S0n0'])
                for b in range(16):
                    bs = b % 2
                    if b + 1 < 16:
                        yield S.dma('sp', v3(S0n[1 - bs], 4), sts[b + 1][g * 512:(g + 1) * 512].rearrange('(j p) n -> p j n', p=128),
                              w=['S0n%d' % (1 - bs)])
                    bT, nT = bank()
                    for j in range(4):
                        yield S.op('pe', tr(bT[:, jsl[j]], S0n[bs][:, jsl[j]], identf), r=['S0n%d' % bs, 'cst'], w=[nT], inc=(j == 3))
                    yield S.op('act', actf(S0T[bs], bT, AF.Copy), r=[nT], w=['S0T'])
                    release(nT)
                    for j in range(4):
                        yield S.op('pe', mm(bJ[:, j * 128 + 8 * b:j * 128 + 8 * b + 8], S0T[bs][:, jsl[j]], xabv[:, 5, 8 * b:8 * b + 8]),
                             r=['S0T', R('xab')], w=[nJ], inc=(j == 3))
                    yield S.op('dve', ts(xdtb[bs], xdtte[p], rowm[:, b:b + 1], None, ALU.mult), r=[R('xdtte'), 'cst'],
                         w=['xdtb'])
                    bQ, nQ = bank()
                    for j in range(4):
                        yield S.op('pe', mm(bQ[:, jsl[j]], xdtb[bs][:, jsl[j]], Btok[p]), r=['xdtb', R('Btok')], w=[nQ],
                             inc=(j == 3))
                    for j in range(4):
                        yield S.op('dve', stt(S0n[bs][:, jsl[j]], S0n[bs][:, jsl[j]], fsT[p][:, j * 128 + 8 * b + 7:j * 128 + 8 * b + 8],
                                        bQ[:, jsl[j]], ALU.mult, ALU.add), r=['S0n%d' % bs, R('fsT'), nQ], w=['S0n%d' % bs])
                    release(nQ)
                    yield S.dma('sp', sss[b][g * 512:(g + 1) * 512].rearrange('(j p) n -> p j n', p=128), v3(S0n[bs], 4),
                          r=['S0n%d' % bs], w=['sss_out%d' % next(uid)])
            yield S.op('dve', tt(tmp[p], bJ, fsT[p], ALU.mult), r=[nJ, R('fsT')], w=[R('tmp')])
            release(nJ)
            bK, nK = bank()
            for j in range(4):
                for e in range(2):
                    h = 2 * j + e
                    yield S.op('pe', mm(bK[:, jsl[j]], xdtpad[p][:, jsl[h]], MT[p][:, jsl[h]], start=(e == 0), stop=(e == 1)),
                         r=[R('xdtpad'), R('MT')], w=[nK], inc=(e == 1 and j == 3))
            yield S.op('dve', tt(tmp[p], tmp[p], bK, ALU.add), r=[R('tmp'), nK], w=[R('tmp')])
            release(nK)
            for j in range(4):
                yield S.op('dve', stt(tmp[p][:, jsl[j]], acc[p][:, jsl[j]], dcol[:, j:j + 1], tmp[p][:, jsl[j]], ALU.mult, ALU.add),
                     r=accn + ['mcp', R('tmp')], w=[R('tmp')])
            yield S.op('dve', tt(tmp[p], tmp[p], sz[p], ALU.mult), r=[R('tmp'), R('sz')], w=[R('tmp')])
            yield S.op('act', actf(ysq[p], tmp[p], AF.Square), r=[R('tmp')], w=[R('ysq')])
            bL, nL = bank()
            for j in range(4):
                yield S.op('pe', mm(bL[:, 0:128], onesb, ysq[p][:, jsl[j]], start=(j == 0), stop=(j == 3)), r=['onesb', R('ysq')],
                     w=[nL], inc=(j == 3))
            yield S.op('act', actf(rstdy[p], bL[:, 0:128], AF.Ln, scale=1.0 / 512, bias=epsc[:, 0:1]), r=[nL, 'epsc'], w=[R('rstdy')])
            release(nL)
            yield S.op('act', actf(rstdy[p], rstdy[p], AF.Exp, scale=-0.5), r=[R('rstdy')], w=[R('rstdy')])
            for j in range(4):
                yield S.op('dve', stt(mix2[p][:, jsl[j]], tmp[p][:, jsl[j]], mncol[:, g * 4 + j:g * 4 + j + 1], rstdy[p],
                                ALU.mult, ALU.mult), r=[R('tmp'), 'mncol', R('rstdy')], w=[R('mix2')])
            mixv = v3(mix2[p], 4)
            for n in range(2):
                bN, nN = bank()
                for c in range(4):
                    yield S.op('pe', mm(bN, mixv[:, c, :], WoMv[:, c, n * 512:(n + 1) * 512], start=(c == 0), stop=(c == 3)),
                         r=[R('mix2'), 'WoM'], w=[nN], inc=(c == 3))
                xs_ = x1t(T)[:, n * 512:(n + 1) * 512]
                yield S.op('dve', tt(xs_, bN, xs_, ALU.add), r=[nN, 'x1_%d' % T], w=['x1_%d' % T])
                release(nN)

        def chain(T):
            yield from front(T)
            if T == ORDER[-1]:
                prefetch_after('M%d' % g)
            yield from back(T)

        run_chains(chain, stagger=STAG_M)

    phase_M(0)
    if stop_after == 'M0':
        return dump_x1()
    phase_M(1)

    if stop_after == 'A2':
        for T in range(NT):
            dst = yp[T * 128:(T + 1) * 128, :] if T < NTP else ys
            S.dma('sp', dst, x1t(T), r=['x1_%d' % T], w=['y_out%d' % next(uid)])
        S.finish()
        S.emit()
        return nc

    S.barrier()
    ar.off = persist_mark
    h2v = hTv_all
    NQ = 8
    Wu = [ar.bf(8 * 512) for _ in range(2)]; Wd = [ar.bf(4 * 1024) for _ in range(2)]
    Wuv = [w.rearrange('p (c f) -> p c f', c=8) for w in Wu]
    Wdv = [w.rearrange('p (j d) -> p j d', j=4) for w in Wd]
    uT = [ar.bf(4 * 512) for _ in range(2)]
    sq = [ar.f32(512) for _ in range(2)]
    xn = ar.bf(1024); junk = ar.f32(1024)
    lnf_bc = ar.f32(1024)
    yo = [ar.f32(1024) for _ in range(2)]
    print('C arena words', ar.off, 'of', ar.n)
    S.dma('sp', lnf_bc, bc_ap(lnf_h, 1024), w=['lnf_bc'])

    def load_q(Q):
        s_ = Q % 2
        if not (Q < 2 and 'Wu' in prefetched):
            S.dma('pool', Wu[s_], wu_h.ap()[Q], w=['Wu%d' % s_])
        S.dma('pool', Wd[s_], wd_h.ap()[Q], w=['Wd%d' % s_])

    load_q(0)
    load_q(1)
    for T in range(NT):
        rms_to_hT(x1[:, T * D:(T + 1) * D], 'x1_%d' % T, xn, 'cxn', h2v[:, :, T * 128:(T + 1) * 128], 'h2T_%d' % T,
                  g2col, 'g2col', junk, 'cjunk')
    held.clear()
    PUP = [ps_t[:, i * 512:(i + 1) * 512] for i in range(4)]
    PDN = [ps_t[:, 2048:3072], ps_t[:, 3072:4096]]
    groups = [(0, 4), (4, 4), (8, 4), (12, 4), (16, 1)]
    seq = [(Q, gi) for Q in range(NQ) for gi in range(len(groups))]
    cnt = {'up': 0, 'dn': 0}

    def up(i):
        Q, gi = seq[i]
        t0, nt = groups[gi]
        s_ = Q % 2
        ntok = nt * 128
        us = i % 2
        uv = uT[us].rearrange('p (j t) -> p j t', j=4)
        hres = ['h2T_%d' % t for t in range(t0, t0 + nt)]
        for j in range(4):
            pb = cnt['up'] % 4; cnt['up'] += 1
            for c_ in range(8):
                S.op('pe', mm(PUP[pb][:, 0:ntok], Wuv[s_][:, c_, j * 128:(j + 1) * 128],
                              h2v[:, c_, t0 * 128:t0 * 128 + ntok], start=(c_ == 0), stop=(c_ == 7)),
                     r=hres + ['Wu%d' % s_], w=['PS%d' % pb], inc=(c_ == 7))
            S.op('act', actf(sq[j % 2][:, 0:ntok], PUP[pb][:, 0:ntok], AF.Square), r=['PS%d' % pb], w=['sq%d' % (j % 2)])
            S.op('dve', stt(uv[:, j, 0:ntok], PUP[pb][:, 0:ntok], 0.0, sq[j % 2][:, 0:ntok], ALU.is_gt, ALU.mult),
                 r=['PS%d' % pb, 'sq%d' % (j % 2)], w=['uT%d' % us])

    def down(i):
        Q, gi = seq[i]
        t0, nt = groups[gi]
        s_ = Q % 2
        us = i % 2
        uv = uT[us].rearrange('p (j t) -> p j t', j=4)
        for t in range(nt):
            T = t0 + t
            db = cnt['dn'] % 2; cnt['dn'] += 1
            for n in range(2):
                for j in range(4):
                    S.op('pe', mm(PDN[db][:, n * 512:(n + 1) * 512], uv[:, j, t * 128:(t + 1) * 128],
                                  Wdv[s_][:, j, n * 512:(n + 1) * 512], start=(j == 0), stop=(j == 3)),
                         r=['uT%d' % us, 'Wd%d' % s_], w=['PS%d' % (4 + 2 * db + n)], inc=(j == 3))
            xT = x1[:, T * D:(T + 1) * D]
            S.op('dve', tt(xT, PDN[db], xT, ALU.add), r=['PS%d' % (4 + 2 * db), 'PS%d' % (5 + 2 * db), 'x1_%d' % T],
                 w=['x1_%d' % T])
            if Q == NQ - 1:
                ys_ = T % 2
                S.op('dve', (lambda v, xT=xT: v.scalar_tensor_tensor(
                    out=junk, in0=xT, scalar=1.0, in1=xT, op0=ALU.mult, op1=ALU.mult, accum_out=rstd_s[:, 4:5])),
                    r=['x1_%d' % T], w=['cjunk', 'fssq'])
                S.op('act', actf(rstd_s[:, 5:6], rstd_s[:, 4:5], AF.Ln, scale=1.0 / D, bias=epsc[:, 0:1]),
                     r=['fssq', 'epsc'], w=['flnv'])
                S.op('act', actf(rstd_s[:, 6:7], rstd_s[:, 5:6], AF.Exp, scale=-0.5), r=['flnv'], w=['frstd'])
                S.op('dve', stt(yo[ys_], xT, rstd_s[:, 6:7], lnf_bc, ALU.mult, ALU.mult),
                     r=['x1_%d' % T, 'frstd', 'lnf_bc'], w=['yo%d' % ys_])
                dst = yp[T * 128:(T + 1) * 128, :] if T < NTP else ys
                S.dma('sp', dst, yo[ys_], r=['yo%d' % ys_], w=['y_out%d' % next(uid)])
        if gi == len(groups) - 1 and Q + 2 < NQ:
            load_q(Q + 2)

    up(0)
    for i in range(len(seq)):
        if i + 1 < len(seq):
            up(i + 1)
        down(i)
    S.finish()
    S.emit()
    return nc


def shard_inputs(inp):
    f = lambda a: np.ascontiguousarray(np.asarray(a, dtype=np.float32))
    cst = make_consts()
    w_in = np.asarray(inp['w_in'][0], np.float32)
    w_out = np.asarray(inp['w_out'][0], np.float32)
    w_up = np.asarray(inp['w_up'][0], np.float32)
    w_dn = np.asarray(inp['w_down'][0], np.float32)
    Wp = w_in.reshape(8, 128, DIN).transpose(1, 0, 2)
    w_h = np.stack([np.stack([Wp[:, :, k * 1024 + hb * 512:k * 1024 + (hb + 1) * 512] for k in range(4)], axis=2)
                    .reshape(128, 16384) for hb in range(2)])
    wo_h = np.stack([w_out[hb * 512:(hb + 1) * 512].reshape(4, 128, 1024).transpose(1, 0, 2).reshape(128, 4096)
                     for hb in range(2)])
    w_m = np.stack([np.concatenate([Wp[:, :, 4096 + g * 512:4096 + (g + 1) * 512], Wp[:, :, 5120 + g * 512:5120 + (g + 1) * 512],
                                    Wp[:, :, 6144 + g * 128:6144 + (g + 1) * 128], Wp[:, :, 6400 + g * 128:6400 + (g + 1) * 128],
                                    Wp[:, :, 6656:6672]], axis=2).reshape(128, 8 * 1296) for g in range(2)])
    wo_m = np.stack([w_out[1024 + g * 512:1024 + (g + 1) * 512].reshape(4, 128, 1024).transpose(1, 0, 2).reshape(128, 4096)
                     for g in range(2)])
    Up = w_up.reshape(8, 128, 4096).transpose(1, 0, 2)
    w_u = np.stack([Up[:, :, Q * 512:(Q + 1) * 512].reshape(128, 4096) for Q in range(8)])
    w_d = np.stack([w_dn[Q * 512:(Q + 1) * 512].reshape(4, 128, 1024).transpose(1, 0, 2).reshape(128, 4096) for Q in range(8)])
    cw = np.asarray(inp['conv_w'][0], np.float32); cb = np.asarray(inp['conv_b'][0], np.float32)
    dsk = np.asarray(inp['d_skip'][0], np.float32); dtb_ = np.asarray(inp['dt_bias'][0], np.float32)
    alog_ = np.asarray(inp['a_log'][0], np.float32)
    mcst = np.zeros((128, 100), np.float32)
    for g in range(2):
        choff = [g * 512 + j * 128 for j in range(4)] + [1024 + g * 128, 1280 + g * 128]
        o = g * 50
        for blk in range(6):
            for k in range(4):
                mcst[:, o + blk * 4 + k] = cw[k, choff[blk]:choff[blk] + 128]
            mcst[:, o + 24 + blk] = cb[choff[blk]:choff[blk] + 128]
        for j in range(4):
            for e in range(2):
                mcst[e * 64:(e + 1) * 64, o + 30 + j] = dsk[8 * g + 2 * j + e]
        mcst[:, o + 34:o + 42] = dtb_[8 * g:8 * g + 8][None, :]
        mcst[:, o + 42:o + 50] = alog_[8 * g:8 * g + 8][None, :]
    shared = {
        'lbl': f(inp['hg_lb_logits']), 'ln1': f(inp['ln1'][0]),
        'w_h': f(w_h), 'wo_h': f(wo_h), 'w_m': f(w_m), 'wo_m': f(wo_m), 'w_u': f(w_u), 'w_d': f(w_d),
        'hgn': f(inp['hg_norm'][0].reshape(1024)), 'mcst': mcst,
        'm_norm': f(inp['m_norm'][0]), 'ln2': f(inp['ln2'][0]), 'ln_f': f(inp['ln_f']), 'cst': cst,
    }
    maps = []
    for i in range(NCORES):
        m = dict(shared)
        m.update({
            'xp': f(inp['x_prompt'][i]),
            'xs': f(inp['x_sample'][16 * i:16 * i + 16].reshape(128, D)),
            'st_h': f(inp['state_hgrn'][0, 16 * i:16 * i + 16]),
            'st_s': f(inp['state_ssm'][0, 16 * i:16 * i + 16].reshape(16, 1024, 128)),
            'st_c': f(inp['state_conv'][0, 16 * i:16 * i + 16].reshape(48, 1536)),
        })
        maps.append(m)
    return maps


def gather(res):
    R = res.results
    y_prompt = np.stack([R[i]['yp'] for i in range(NCORES)]).astype(np.float32)
    y_sample = np.concatenate([R[i]['ys'].reshape(16, 8, D) for i in range(NCORES)]).astype(np.float32)
    hgp = np.stack([R[i]['hgp'] for i in range(NCORES)])[None].astype(np.float32)
    hgs = np.concatenate([R[i]['hgs'] for i in range(NCORES)])[None].astype(np.float32)
    ssp = np.stack([R[i]['ssp'].reshape(16, 64, 128) for i in range(NCORES)])[None].astype(np.float32)
    sss = np.concatenate([R[i]['sss'].reshape(16, 16, 64, 128) for i in range(NCORES)])[None].astype(np.float32)
    cvp = np.stack([R[i]['cvp'] for i in range(NCORES)])[None].astype(np.float32)
    cvs = np.concatenate([R[i]['cvs'].reshape(16, 3, 1536) for i in range(NCORES)])[None].astype(np.float32)
    return (y_prompt, y_sample, hgp, hgs, ssp, sss, cvp, cvs)


def kernel(**inputs):
    nc = build()
    maps = shard_inputs(inputs)
    res = run_bass_kernel_spmd(nc, maps, core_ids=list(range(NCORES)))
    return gather(res)
```

```python
import numpy as np
from contextlib import ExitStack
import concourse.bass as bass
import concourse.mybir as mybir
from concourse.bass_utils import run_bass_kernel_spmd

F32 = mybir.dt.float32
BF = mybir.dt.bfloat16
AF = mybir.ActivationFunctionType
ALU = mybir.AluOpType

NCORES = 8
D = 1024
L = 2048
NTP = 16
NT = 17
DIN = 6672
EPS = 1e-5
ENG = ('pe', 'act', 'dve', 'pool', 'sp')


class Sch:
    def __init__(self, nc, es, ndsem=(('sp', 14), ('pool', 6), ('act', 4))):
        self.nc = nc
        self.prog = {e: [] for e in ENG}
        self.cnt = {e: 0 for e in ENG}
        self.sem = {e: es.enter_context(nc.semaphore('prog_' + e)) for e in ENG}
        self.waited = {e: {} for e in ENG}
        self.lastw = {}
        self.readers = {}
        self.dsem = {}
        self.duse = {}
        self.dnext = {}
        for q, n in ndsem:
            self.dsem[q] = [es.enter_context(nc.semaphore('d_%s_%d' % (q, i))) for i in range(n)]
            self.duse[q] = [0] * n
            self.dnext[q] = 0

    def _need(self, eng, ev):
        kind, key, val = ev
        if kind == 'e' and key == eng and eng in ('pe', 'sp'):
            return
        k = (kind, key)
        if val > self.waited[eng].get(k, 0):
            self.prog[eng].append(('wait', k, val))
            self.waited[eng][k] = val

    def _deps(self, eng, r, w):
        for res in r:
            if res in self.lastw:
                self._need(eng, self.lastw[res])
            if res[0] == 'P':
                for k, val in self.readers.get(res, {}).items():
                    if k != ('e', eng):
                        self._need(eng, (k[0], k[1], val))
        for res in w:
            if res in self.lastw:
                self._need(eng, self.lastw[res])
            for k, val in self.readers.get(res, {}).items():
                self._need(eng, (k[0], k[1], val))

    def _record(self, ev, r, w):
        k = (ev[0], ev[1])
        for res in r:
            d = self.readers.setdefault(res, {})
            if ev[2] > d.get(k, 0):
                d[k] = ev[2]
        for res in w:
            self.lastw[res] = ev
            self.readers[res] = {}

    def op(self, eng, fn, r=(), w=(), inc=True):
        self._deps(eng, r, w)
        n = self.cnt[eng] + 1
        if inc:
            self.cnt[eng] = n
        import sys as _s
        self.prog[eng].append(('op', fn, inc, _s._getframe(1).f_lineno, (tuple(r), tuple(w))))
        self._record(('e', eng, n), r, w)
        return eng

    def dma(self, q, out, in_, r=(), w=(), **kw):
        i = self.dnext[q]
        self.dnext[q] = (i + 1) % len(self.dsem[q])
        uses = self.duse[q][i]
        if uses > 0:
            self._need(q, ('d', (q, i), 16 * uses))
        self._deps(q, r, w)
        self.duse[q][i] = uses + 1
        self.prog[q].append(('dma', out, in_, (q, i), kw))
        self._record(('d', (q, i), 16 * (uses + 1)), r, w)

    def barrier(self):
        for e in ENG:
            for e2 in ENG:
                if e2 != e and self.cnt[e2] > 0:
                    self._need(e, ('e', e2, self.cnt[e2]))
            for q in self.dsem:
                for i, u in enumerate(self.duse[q]):
                    if u > 0:
                        self._need(e, ('d', (q, i), 16 * u))

    def finish(self):
        for q in self.dsem:
            for i, u in enumerate(self.duse[q]):
                if u > 0:
                    self._need('sp', ('d', (q, i), 16 * u))
        for e2 in ENG:
            if e2 != 'sp' and self.cnt[e2] > 0:
                self._need('sp', ('e', e2, self.cnt[e2]))

    def _semof(self, k):
        return self.sem[k[1]] if k[0] == 'e' else self.dsem[k[1][0]][k[1][1]]

    def check(self):
        pc = {e: 0 for e in ENG}
        val = {}
        progress = True
        while progress:
            progress = False
            for e in ENG:
                while pc[e] < len(self.prog[e]):
                    it = self.prog[e][pc[e]]
                    if it[0] == 'wait':
                        if val.get(it[1], 0) >= it[2]:
                            pc[e] += 1; progress = True
                        else:
                            break
                    elif it[0] == 'op':
                        if it[2]:
                            val[('e', e)] = val.get(('e', e), 0) + 1
                        pc[e] += 1; progress = True
                    else:
                        k = ('d', it[3])
                        val[k] = val.get(k, 0) + 16
                        pc[e] += 1; progress = True
        stuck = {e: (pc[e], len(self.prog[e]), self.prog[e][pc[e]][:3] if pc[e] < len(self.prog[e]) else None,
                     [it[3:] for it in self.prog[e][pc[e]:pc[e] + 3] if it[0] == 'op']) for e in ENG}
        ok = all(pc[e] == len(self.prog[e]) for e in ENG)
        if not ok:
            print('DEADLOCK CHECK STUCK', stuck)
        return ok

    def emit(self):
        nc = self.nc
        if not self.check():
            raise RuntimeError('semaphore schedule would deadlock')
        with nc.Block() as block:
            for e, reg in (('sp', block.sync), ('pe', block.tensor), ('act', block.scalar),
                           ('dve', block.vector), ('pool', block.gpsimd)):
                def body(eo, e=e):
                    for it in self.prog[e]:
                        if it[0] == 'wait':
                            eo.wait_ge(self._semof(it[1]), it[2])
                        elif it[0] == 'op':
                            ins = it[1](eo)
                            if it[2]:
                                ins.then_inc(self.sem[e], 1)
                        else:
                            eo.dma_start(out=it[1], in_=it[2], **it[4]).then_inc(
                                self.dsem[it[3][0]][it[3][1]], 16)
                reg(body)


class Arena:
    def __init__(self, t, nwords):
        self.t = t
        self.n = nwords
        self.off = 0

    def f32(self, n):
        assert self.off + n <= self.n, ('arena overflow', self.off, n, self.n)
        v = self.t[:, self.off:self.off + n]
        self.off += n
        return v

    def bf(self, n):
        w = (n + 1) // 2
        return self.f32(w).bitcast(BF)


def mm(out, lhsT, rhs, start=True, stop=True):
    return lambda pe: pe.matmul(out, lhsT, rhs, start=start, stop=stop)


def tr(out, in_, ident):
    return lambda pe: pe.transpose(out, in_, ident)


def actf(out, in_, func, **kw):
    return lambda a: a.activation(out=out, in_=in_, func=func, **kw)


def tt(out, a, b, op):
    return lambda v: v.tensor_tensor(out=out, in0=a, in1=b, op=op)


def ts(out, a, s1, s2, op0, op1=ALU.bypass):
    if s2 is None:
        return lambda v: v.tensor_scalar(out=out, in0=a, scalar1=s1, scalar2=None, op0=op0)
    return lambda v: v.tensor_scalar(out=out, in0=a, scalar1=s1, scalar2=s2, op0=op0, op1=op1)


def stt(out, in0, scalar, in1, op0, op1):
    return lambda v: v.scalar_tensor_tensor(out=out, in0=in0, scalar=scalar, in1=in1, op0=op0, op1=op1)


def cp(out, in_):
    return lambda v: v.tensor_copy(out=out, in_=in_)


def v3(ap, h=8):
    return ap.rearrange('p (h t) -> p h t', h=h)


C_IDENT, C_TRIP, C_TRIS, C_STRP, C_STRS, C_ONES, C_ROWM = 0, 128, 256, 384, 512, 640, 768
C_TOT = 784


def make_consts():
    c = np.zeros((128, C_TOT), np.float32)
    s = np.arange(128)[:, None]
    t = np.arange(128)[None, :]
    same = (s // 8) == (t // 8)
    c[:, C_IDENT:C_IDENT + 128] = (s == t)
    c[:, C_TRIP:C_TRIP + 128] = (s <= t)
    c[:, C_TRIS:C_TRIS + 128] = (s <= t) & same
    c[:, C_STRP:C_STRP + 128] = (s > t)
    c[:, C_STRS:C_STRS + 128] = (s > t) & same
    c[:, C_ONES:C_ONES + 128] = 1.0
    c[:, C_ROWM:C_ROWM + 16] = (s // 8) == np.arange(16)[None, :]
    return c


STAG_H = 0
STAG_M = 0


def build(stop_after=None, dbg=False):
    nc = bass.Bass('TRN2', target_bir_lowering=False)
    es = ExitStack()

    def din(name, shape):
        return nc.dram_tensor(name, list(shape), F32, kind='ExternalInput')

    def dout(name, shape):
        return nc.dram_tensor(name, list(shape), F32, kind='ExternalOutput')

    xp_h = din('xp', (L, D)); xs_h = din('xs', (128, D))
    sth_h = din('st_h', (16, 8, 128, 128)); sts_h = din('st_s', (16, 1024, 128)); stc_h = din('st_c', (48, 1536))
    lbl_h = din('lbl', (2, 1024)); ln1_h = din('ln1', (1024,))
    wh_h = din('w_h', (2, 128, 16384)); woh_h = din('wo_h', (2, 128, 4096))
    wm_h = din('w_m', (2, 128, 8 * 1296)); wom_h = din('wo_m', (2, 128, 4096))
    wu_h = din('w_u', (8, 128, 4096)); wd_h = din('w_d', (8, 128, 4096))
    hgn_h = din('hgn', (1024,)); mcst_h = din('mcst', (128, 100))
    mn_h = din('m_norm', (1024,)); ln2_h = din('ln2', (1024,)); lnf_h = din('ln_f', (1024,))
    cst_h = din('cst', (128, C_TOT))
    yp_h = dout('yp', (L, D)); ys_h = dout('ys', (128, D))
    hgp_h = dout('hgp', (8, 128, 128)); hgs_h = dout('hgs', (16, 8, 128, 128))
    ssp_h = dout('ssp', (1024, 128)); sss_h = dout('sss', (16, 1024, 128))
    cvp_h = dout('cvp', (3, 1536)); cvs_h = dout('cvs', (48, 1536))
    xp, xs, sth, sts, stc = xp_h.ap(), xs_h.ap(), sth_h.ap(), sts_h.ap(), stc_h.ap()
    yp, ys, hgp, hgs, ssp, sss, cvp, cvs = (yp_h.ap(), ys_h.ap(), hgp_h.ap(), hgs_h.ap(),
                                             ssp_h.ap(), sss_h.ap(), cvp_h.ap(), cvs_h.ap())
    dbg_h = dout('dbg', (128, NT, 2048)) if dbg else None

    AW = 52800
    arena_t = es.enter_context(nc.sbuf_tensor('arena', [128, AW], F32))
    ar = Arena(arena_t, AW)
    ps_t = es.enter_context(nc.psum_tensor('psum', [128, 4096], F32))
    P1 = ps_t[:, 0:1024]; P2 = ps_t[:, 1024:2048]; P3 = ps_t[:, 2048:3072]
    PT = ps_t[:, 3072:3584].bitcast(BF)
    P4 = ps_t[:, 3584:4096]
    S = Sch(nc, es)

    def col_ap(h, n_c):
        return bass.AP(h, 0, [[1, 128], [128, n_c]])

    def bc_ap(h, n, off=0):
        return bass.AP(h, off, [[0, 128], [1, n]])

    cst = ar.f32(C_TOT)
    x1 = ar.f32(NT * D)
    hTall = ar.bf(8 * NT * 128); hTv_all = hTall.rearrange('p (c t) -> p c t', c=8)
    identb = ar.bf(128); onesb = ar.bf(128)
    g1col = ar.f32(8); hgncol = ar.f32(8); mncol = ar.f32(8); g2col = ar.f32(8)
    rstd_s = ar.f32(8)
    epsc = ar.f32(8)
    mcp = ar.f32(100)
    S.dma('sp', cst, cst_h.ap(), w=['cst'])
    S.dma('sp', g1col, col_ap(ln1_h, 8), w=['g1col'], allow_slow_non_contiguous=True)
    S.dma('sp', hgncol, col_ap(hgn_h, 8), w=['hgncol'], allow_slow_non_contiguous=True)
    S.dma('sp', mncol, col_ap(mn_h, 8), w=['mncol'], allow_slow_non_contiguous=True)
    S.dma('sp', g2col, col_ap(ln2_h, 8), w=['g2col'], allow_slow_non_contiguous=True)
    S.dma('sp', mcp, mcst_h.ap(), w=['mcp'])
    identf = cst[:, C_IDENT:C_IDENT + 128]
    onesf = cst[:, C_ONES:C_ONES + 128]
    rowm = cst[:, C_ROWM:C_ROWM + 16]
    S.op('dve', cp(identb, identf), r=['cst'], w=['identb'])
    S.op('dve', cp(onesb, onesf), r=['cst'], w=['onesb'])
    S.op('pool', lambda g: g.memset(epsc, EPS), w=['epsc'])
    persist_mark = ar.off

    held = set()
    bstate = [0]

    def bank(hold=True):
        for _ in range(16):
            i = bstate[0] % 8
            bstate[0] += 1
            if i not in held:
                if hold:
                    held.add(i)
                return ps_t[:, i * 512:(i + 1) * 512], 'PS%d' % i
        raise RuntimeError('no free PSUM bank')

    def release(name):
        held.discard(int(name[2:]))

    def run_chains(chain, K=2, stagger=0):
        active = []
        free = list(range(K))
        nxt = 0
        since = 10 ** 9
        while active or nxt < NT:
            while len(active) < K and nxt < NT and (not active or since >= stagger):
                T_ = ORDER[nxt]
                PAR[T_] = free.pop(0)
                active.append((chain(T_), T_))
                nxt += 1
                since = 0
            since += 1
            for it_ in list(active):
                try:
                    while next(it_[0]) == 'pe':
                        pass
                except StopIteration:
                    active.remove(it_)
                    free.append(PAR[it_[1]])

    ORDER = list(range(NT))
    PAR = {}

    def rms_to_hT(xt, xtag, xn, xnres, hTv, hres, gcol, gres, junk, jres, on_act=False):
        if on_act:
            S.op('act', actf(junk, xt, AF.Square, accum_out=rstd_s[:, 0:1]), r=[xtag], w=[jres, 'ssq'])
        else:
            S.op('dve', (lambda v, xt=xt, junk=junk: v.scalar_tensor_tensor(
                out=junk, in0=xt, scalar=1.0, in1=xt, op0=ALU.mult, op1=ALU.mult, accum_out=rstd_s[:, 0:1])),
                r=[xtag], w=[jres, 'ssq'])
        S.op('act', actf(rstd_s[:, 1:2], rstd_s[:, 0:1], AF.Ln, scale=1.0 / D, bias=epsc[:, 0:1]),
             r=['ssq', 'epsc'], w=['lnv'])
        S.op('act', actf(rstd_s[:, 2:3], rstd_s[:, 1:2], AF.Exp, scale=-0.5), r=['lnv'], w=['rstd'])
        S.op('act', actf(xn, xt, AF.Copy, scale=rstd_s[:, 2:3]), r=[xtag, 'rstd'], w=[xnres])
        bk, bn = bank()
        bkb = bk.bitcast(BF)
        for c in range(8):
            S.op('pe', tr(bkb[:, c * 128:(c + 1) * 128], xn[:, c * 128:(c + 1) * 128], identb),
                 r=[xnres, 'identb'], w=[bn], inc=(c == 7))
        for c in range(8):
            S.op('dve', ts(hTv[:, c, :], bkb[:, c * 128:(c + 1) * 128], gcol[:, c:c + 1], None, ALU.mult),
                 r=[bn, gres], w=[hres])
        release(bn)

    def xsrc(T):
        return xp[T * 128:(T + 1) * 128, :] if T < NTP else xs

    def x1t(T):
        return x1[:, T * D:(T + 1) * D]

    for T in ORDER[:3]:
        S.dma('sp', x1t(T), xsrc(T), w=['x1_%d' % T])

    import itertools
    uid = itertools.count()
    prefetched = set()

    def wview(off, nbf):
        return arena_t[:, persist_mark + off:persist_mark + off + nbf // 2].bitcast(BF)

    def prefetch_after(phase):
        allH = ['WH%d' % k for k in range(8)]
        allM = ['WM%d' % k for k in range(8)]
        if phase == 'H0':
            for c in range(8):
                S.dma('pool', wview(c * 1024, 2048), wh_h.ap()[1][:, c * 2048:(c + 1) * 2048], w=['WH%d' % c])
            prefetched.add('H1')
        elif phase == 'H1':
            for c in range(8):
                S.dma('pool', wview(c * 648, 1296), wm_h.ap()[0][:, c * 1296:(c + 1) * 1296], w=allH + ['WM%d' % c])
            S.dma('pool', wview(5184, 4096), wom_h.ap()[0], w=allH + ['WoM'])
            prefetched.add('M0'); prefetched.add('WoM0')
        elif phase == 'M0':
            for c in range(8):
                S.dma('pool', wview(c * 648, 1296), wm_h.ap()[1][:, c * 1296:(c + 1) * 1296], w=['WM%d' % c])
            prefetched.add('M1')
        elif phase == 'M1':
            for s_ in range(2):
                S.dma('pool', wview(s_ * 2048, 4096), wu_h.ap()[s_], w=allM + ['Wu%d' % s_])
            prefetched.add('Wu')

    def phase_H(hb):
        S.barrier()
        ar.off = persist_mark
        WH = ar.bf(8 * 2048); WHv = WH.rearrange('p (c k e) -> p c k e', c=8, k=4)
        WoH = ar.bf(4 * 1024); WoHv = WoH.rearrange('p (c e) -> p c e', c=4)
        if ('H%d' % hb) not in prefetched:
            for c in range(8):
                S.dma('pool', WH[:, c * 2048:(c + 1) * 2048], wh_h.ap()[hb][:, c * 2048:(c + 1) * 2048], w=['WH%d' % c])
        S.dma('pool', WoH, woh_h.ap()[hb], w=['WoH'])
        oml = ar.f32(512); lbt = ar.f32(1024)
        S.dma('sp', lbt.rearrange('p (a n) -> p a n', a=2), bass.AP(lbl_h, hb * 512, [[0, 128], [1024, 2], [1, 512]]),
              w=['S0f0', 'S0f1'])
        S.op('dve', tt(oml, lbt[:, 0:512], lbt[:, 512:1024], ALU.subtract), r=['S0f0', 'S0f1'], w=['oml'])
        S.op('act', actf(oml, oml, AF.Exp), r=['oml'], w=['oml'])
        S.op('act', actf(oml, oml, AF.Ln, bias=1.0), r=['oml'], w=['oml'])
        S.op('act', actf(oml, oml, AF.Exp, scale=-1.0), r=['oml'], w=['oml'])
        f2 = lambda: [ar.f32(512), ar.f32(512)]
        b2 = lambda: [ar.bf(512), ar.bf(512)]
        kf, logf, sgg = f2(), f2(), f2()
        vtok, ktok = b2(), b2()
        eo, Epos, er = f2(), f2(), f2()
        kend, qdT, kdT, attT, osq, mixT = b2(), b2(), b2(), b2(), b2(), b2()
        St = [ar.f32(512), ar.f32(512)]; Sbf = [ar.bf(512) for _ in range(3)]
        S0f = [lbt[:, 0:512], lbt[:, 512:1024]]; S0b = b2(); kendb = b2(); oc = ar.f32(512)
        print('H arena words', ar.off, 'of', ar.n)
        S.op('pool', lambda g: g.memset(St[0], 0.0), w=['St0'])
        S.op('pool', lambda g: g.memset(St[1], 0.0), w=['St1'])
        S.op('pool', lambda g: g.memset(Sbf[0], 0.0), w=['Sbf0'])
        qbank = {}
        hsl = [slice(h * 128, (h + 1) * 128) for h in range(4)]

        def front(T):
            p = PAR[T]
            q = T % 2
            hv = hTv_all[:, :, T * 128:(T + 1) * 128]
            hres = 'hT_%d' % T
            if hb == 0:
                if T + 3 < NT:
                    yield S.dma('sp', x1t(T + 3), xsrc(T + 3), w=['x1_%d' % (T + 3)])
                rms_to_hT(x1t(T), 'x1_%d' % T, er[p].bitcast(BF), 'er%d' % p, hv, hres, g1col, 'g1col',
                          eo[p].bitcast(BF), 'eo%d' % p, on_act=True)
                yield None
            bA, nA = bank()
            for c in range(8):
                yield S.op('pe', mm(bA, hv[:, c, :], WHv[:, c, 1, :], start=(c == 0), stop=(c == 7)),
                     r=[hres, 'WH%d' % c], w=[nA], inc=(c == 7))
            bB, nB = bank()
            for c in range(8):
                yield S.op('pe', mm(bB, hv[:, c, :], WHv[:, c, 2, :], start=(c == 0), stop=(c == 7)),
                     r=[hres, 'WH%d' % c], w=[nB], inc=(c == 7))
            yield S.op('act', actf(kf[p], bA, AF.Exp), r=[nA], w=['kf%d' % p])
            release(nA)
            yield S.op('act', actf(kf[p], kf[p], AF.Ln, bias=1.0), r=['kf%d' % p], w=['kf%d' % p])
            yield S.op('act', actf(kf[p], kf[p], AF.Exp, scale=-1.0), r=['kf%d' % p], w=['kf%d' % p])
            yield S.op('dve', tt(kf[p], kf[p], oml, ALU.mult), r=['kf%d' % p, 'oml'], w=['kf%d' % p])
            yield S.op('act', actf(logf[p], kf[p], AF.Ln, scale=-1.0, bias=1.0), r=['kf%d' % p], w=['logf%d' % p])
            yield S.op('dve', cp(ktok[p], kf[p]), r=['kf%d' % p], w=['ktok%d' % p])
            yield S.op('act', actf(vtok[p], bB, AF.Copy), r=[nB], w=['vtok%d' % p])
            release(nB)
            bC, nC = bank(hold=True)
            for h in range(4):
                for c in range(8):
                    yield S.op('pe', mm(bC[:, hsl[h]], WHv[:, c, 0, hsl[h]], hv[:, c, :], start=(c == 0), stop=(c == 7)),
                         r=[hres, 'WH%d' % c], w=[nC], inc=(c == 7 and h == 3))
            qbank[T] = (bC, nC)
            bD, nD = bank()
            for h in range(4):
                for c in range(8):
                    yield S.op('pe', mm(bD[:, hsl[h]], WHv[:, c, 3, hsl[h]], hv[:, c, :], start=(c == 0), stop=(c == 7)),
                         r=[hres, 'WH%d' % c], w=[nD], inc=(c == 7 and h == 3))
            yield S.op('act', actf(sgg[p], bD, AF.Exp, scale=-1.0), r=[nD], w=['sgg%d' % p])
            yield S.op('act', actf(sgg[p], sgg[p], AF.Ln, bias=1.0), r=['sgg%d' % p], w=['sgg%d' % p])
            yield S.op('act', actf(sgg[p], sgg[p], AF.Exp, scale=-1.0), r=['sgg%d' % p], w=['sgg%d' % p])
            yield S.op('dve', tt(sgg[p], bD, sgg[p], ALU.mult), r=[nD, 'sgg%d' % p], w=['sgg%d' % p])
            release(nD)

        def back(T):
            p = PAR[T]
            q = T % 2
            samp = (T == NTP)
            c0 = C_TRIS if samp else C_TRIP
            tri = cst[:, c0:c0 + 128]
            c0 = C_STRS if samp else C_STRP
            strict = cst[:, c0:c0 + 128]
            R = lambda n: '%s%d' % (n, p)
            bE, nE = bank()
            yield S.op('pe', mm(bE, strict, logf[p]), r=['cst', R('logf')], w=[nE])
            yield S.op('act', actf(eo[p], bE, AF.Exp), r=[nE], w=[R('eo')])
            release(nE)
            yield S.op('dve', tt(kend[p], ktok[p], eo[p], ALU.mult), r=[R('ktok'), R('eo')], w=[R('kend')])
            bF, nF = bank()
            for h in range(4):
                yield S.op('pe', mm(bF[:, hsl[h]], logf[p][:, hsl[h]], tri), r=['cst', R('logf')], w=[nF], inc=(h == 3))
            yield S.op('act', actf(Epos[p], bF, AF.Exp), r=[nF], w=[R('Epos')])
            yield S.op('act', actf(er[p], bF, AF.Exp, scale=-1.0), r=[nF], w=[R('er')])
            release(nF)
            if not samp:
                bM, nM = bank()
                for h in range(4):
                    yield S.op('pe', mm(bM[:, hsl[h]], kend[p][:, hsl[h]], vtok[p][:, hsl[h]]), r=[R('kend'), R('vtok')],
                         w=[nM], inc=(h == 3))
                for h in range(4):
                    yield S.op('dve', stt(St[1 - q][:, hsl[h]], St[q][:, hsl[h]], Epos[p][:, h * 128 + 127:h * 128 + 128],
                                          bM[:, hsl[h]], ALU.mult, ALU.add), r=['St%d' % q, R('Epos'), nM], w=['St%d' % (1 - q)])
                release(nM)
                yield S.op('act', actf(Sbf[(T + 1) % 3], St[1 - q], AF.Copy), r=['St%d' % (1 - q)], w=['Sbf%d' % ((T + 1) % 3)])
                if T == NTP - 1:
                    yield S.dma('sp', hgp[hb * 4:(hb + 1) * 4].rearrange('h k v -> k h v'), v3(St[1 - q], 4), r=['St%d' % (1 - q)],
                                w=['hgp_out%d' % next(uid)])
            bC, nC = qbank.pop(T)
            yield S.op('dve', tt(qdT[p], bC, Epos[p], ALU.mult), r=[nC, R('Epos')], w=[R('qdT')])
            release(nC)
            bG, nG = bank()
            bGb = bG.bitcast(BF)
            for h in range(4):
                yield S.op('pe', tr(bGb[:, hsl[h]], ktok[p][:, hsl[h]], identb), r=[R('ktok'), 'identb'], w=[nG], inc=(h == 3))
            yield S.op('dve', tt(kdT[p], bGb[:, 0:512], er[p], ALU.mult), r=[nG, R('er')], w=[R('kdT')])
            release(nG)
            bH, nH = bank()
            for h in range(4):
                yield S.op('pe', mm(bH[:, hsl[h]], kdT[p][:, hsl[h]], qdT[p][:, hsl[h]]), r=[R('kdT'), R('qdT')], w=[nH],
                     inc=(h == 3))
            yield S.op('dve', tt(v3(attT[p], 4), v3(bH, 4), tri.unsqueeze(1).to_broadcast([128, 4, 128]), ALU.mult),
                 r=[nH, 'cst'], w=[R('attT')])
            release(nH)
            if not samp:
                bI, nI = bank()
                for h in range(4):
                    yield S.op('pe', mm(bI[:, hsl[h]], vtok[p][:, hsl[h]], attT[p][:, hsl[h]], start=True, stop=False),
                         r=[R('vtok'), R('attT')], w=[nI], inc=False)
                    yield S.op('pe', mm(bI[:, hsl[h]], Sbf[T % 3][:, hsl[h]], qdT[p][:, hsl[h]], start=False, stop=True),
                         r=['Sbf%d' % (T % 3), R('qdT')], w=[nI], inc=(h == 3))
                yield S.op('act', actf(eo[p], bI, AF.Copy), r=[nI], w=[R('eo')])
                release(nI)
            else:
                bP, nP = bank(hold=True)
                yield S.dma('sp', v3(S0f[0], 4), sth[0][hb * 4:(hb + 1) * 4].rearrange('h k v -> k h v'), w=['S0f0'])
                yield S.op('dve', ts(kendb[0], kend[p], rowm[:, 0:1], None, ALU.mult), r=[R('kend'), 'cst'], w=['kendb0'])
                for b in range(16):
                    bs = b % 2
                    if b + 1 < 16:
                        yield S.dma('sp', v3(S0f[1 - bs], 4), sth[b + 1][hb * 4:(hb + 1) * 4].rearrange('h k v -> k h v'),
                              w=['S0f%d' % (1 - bs)])
                    yield S.op('act', actf(S0b[bs], S0f[bs], AF.Copy), r=['S0f%d' % bs], w=['S0b%d' % bs])
                    for h in range(4):
                        yield S.op('pe', mm(bP[:, h * 128 + 8 * b:h * 128 + 8 * b + 8], S0b[bs][:, hsl[h]],
                                      qdT[p][:, h * 128 + 8 * b:h * 128 + 8 * b + 8]),
                             r=['S0b%d' % bs, R('qdT')], w=[nP], inc=(h == 3))
                    if b + 1 < 16:
                        yield S.op('dve', ts(kendb[1 - bs], kend[p], rowm[:, b + 1:b + 2], None, ALU.mult), r=[R('kend'), 'cst'],
                                   w=['kendb%d' % (1 - bs)])
                    bQ, nQ = bank()
                    for h in range(4):
                        yield S.op('pe', mm(bQ[:, hsl[h]], kendb[bs][:, hsl[h]], vtok[p][:, hsl[h]]),
                             r=['kendb%d' % bs, R('vtok')], w=[nQ], inc=(h == 3))
                    for h in range(4):
                        yield S.op('dve', stt(S0f[bs][:, hsl[h]], S0f[bs][:, hsl[h]],
                                        Epos[p][:, h * 128 + 8 * b + 7:h * 128 + 8 * b + 8], bQ[:, hsl[h]], ALU.mult, ALU.add),
                             r=['S0f%d' % bs, R('Epos'), nQ], w=['S0f%d' % bs])
                    release(nQ)
                    yield S.dma('sp', hgs[b][hb * 4:(hb + 1) * 4].rearrange('h k v -> k h v'), v3(S0f[bs], 4),
                          r=['S0f%d' % bs], w=['hgs_out%d' % next(uid)])
                bI, nI = bank()
                for h in range(4):
                    yield S.op('pe', mm(bI[:, hsl[h]], vtok[p][:, hsl[h]], attT[p][:, hsl[h]]), r=[R('vtok'), R('attT')], w=[nI],
                         inc=(h == 3))
                yield S.op('act', actf(oc, bP, AF.Copy), r=[nP], w=['oc'])
                release(nP)
                yield S.op('dve', tt(eo[p], bI, oc, ALU.add), r=[nI, 'oc'], w=[R('eo')])
                release(nI)
            osb = eo[p]
            yield S.op('act', actf(osq[p], osb, AF.Square), r=[R('eo')], w=[R('osq')])
            bJ, nJ = bank()
            yield S.op('pe', mm(bJ, onesb, osq[p]), r=['onesb', R('osq')], w=[nJ])
            yield S.op('act', actf(er[p], bJ, AF.Ln, scale=1.0 / 128, bias=epsc[:, 0:1]), r=[nJ, 'epsc'], w=[R('er')])
            release(nJ)
            yield S.op('act', actf(er[p], er[p], AF.Exp, scale=-0.5), r=[R('er')], w=[R('er')])
            yield S.op('dve', tt(osb, osb, er[p], ALU.mult), r=[R('eo'), R('er')], w=[R('eo')])
            for h in range(4):
                yield S.op('dve', stt(mixT[p][:, hsl[h]], osb[:, hsl[h]], hgncol[:, hb * 4 + h:hb * 4 + h + 1], sgg[p][:, hsl[h]],
                                ALU.mult, ALU.mult), r=[R('eo'), R('sgg'), 'hgncol'], w=[R('mixT')])
            mixv = v3(mixT[p], 4)
            for n in range(2):
                bK, nK = bank()
                for c in range(4):
                    yield S.op('pe', mm(bK, mixv[:, c, :], WoHv[:, c, n * 512:(n + 1) * 512], start=(c == 0), stop=(c == 3)),
                         r=[R('mixT'), 'WoH'], w=[nK], inc=(c == 3))
                xs_ = x1t(T)[:, n * 512:(n + 1) * 512]
                yield S.op('dve', tt(xs_, bK, xs_, ALU.add), r=[nK, 'x1_%d' % T], w=['x1_%d' % T])
                release(nK)

        def chain(T):
            yield from front(T)
            if T == ORDER[-1]:
                prefetch_after('H%d' % hb)
            yield from back(T)

        run_chains(chain, stagger=STAG_H)

    def dump_x1():
        for T in range(NT):
            dst = yp[T * 128:(T + 1) * 128, :] if T < NTP else ys
            S.dma('sp', dst, x1t(T), r=['x1_%d' % T], w=['y_out%d' % next(uid)])
        S.finish()
        S.emit()
        return nc

    phase_H(0)
    if stop_after == 'H0':
        return dump_x1()
    phase_H(1)
    if stop_after == 'H1':
        return dump_x1()

    def phase_M(g):
        S.barrier()
        ar.off = persist_mark
        WM = ar.bf(8 * 1296); WMv = WM.rearrange('p (c e) -> p c e', c=8)
        WoM = ar.bf(4 * 1024); WoMv = WoM.rearrange('p (c e) -> p c e', c=4)
        pieces = [(0, 4096 + g * 512, 512), (512, 5120 + g * 512, 512), (1024, 6144 + g * 128, 128),
                  (1152, 6400 + g * 128, 128), (1280, 6656, 16)]
        if ('M%d' % g) not in prefetched:
            for c in range(8):
                S.dma('pool', WM[:, c * 1296:(c + 1) * 1296], wm_h.ap()[g][:, c * 1296:(c + 1) * 1296], w=['WM%d' % c])
        if ('WoM%d' % g) not in prefetched:
            S.dma('pool', WoM, wom_h.ap()[g], w=['WoM'])
        mo = g * 50
        cwv = mcp[:, mo:mo + 24].rearrange('p (c j) -> p c j', c=6)
        cbt = mcp[:, mo + 24:mo + 30]; dcol = mcp[:, mo + 30:mo + 34]; dtb = mcp[:, mo + 34:mo + 42]
        abc = ar.f32(8)
        choff = [g * 512 + j * 128 for j in range(4)] + [1024 + g * 128, 1280 + g * 128]
        S.op('act', actf(abc, mcp[:, mo + 42:mo + 50], AF.Exp), r=['mcp'], w=['abc'])
        S.op('dve', ts(abc, abc, -1.0, None, ALU.mult), r=['abc'], w=['abc'])
        f2 = lambda n: [ar.f32(n), ar.f32(n)]
        b2 = lambda n: [ar.bf(n), ar.bf(n)]
        XB = f2(6 * 176)
        Xp = [x[:, 0:6 * 131].rearrange('p (c t) -> p c t', c=6) for x in XB]
        Xs = [x.rearrange('p (c b t) -> p c b t', c=6, b=16) for x in XB]
        acc = f2(768); xab = b2(768); sz = f2(512)
        tails = [ar.f32(18).rearrange('p (c t) -> p c t', c=6) for _ in range(3)]
        sm = f2(64)
        one = lambda x: [x, x]
        fsT = f2(512); Lh = f2(1024); MT = b2(1024); cbm = f2(128)
        xdtpad = b2(1024); xdtte = b2(512); Btok = b2(128)
        tmp = f2(512); ysq = b2(512); rstdy = f2(128); mix2 = b2(512)
        STt = [ar.f32(512), ar.f32(512)]; STb = [ar.bf(512) for _ in range(3)]
        S0n_all = ar.f32(1024); S0n = [S0n_all[:, 0:512], S0n_all[:, 512:1024]]; cvo = S0n_all[:, 0:768]
        S0T = b2(512); stcb = ar.f32(768); xdtb = [ar.bf(512), stcb[:, 0:256].bitcast(BF)]
        print('M arena words', ar.off, 'of', ar.n)
        S.op('pool', lambda g_: g_.memset(STt[0], 0.0), w=['STt0'])
        S.op('pool', lambda g_: g_.memset(STt[1], 0.0), w=['STt1'])
        S.op('pool', lambda g_: g_.memset(STb[0], 0.0), w=['STb0'])
        for p_ in range(2):
            S.op('pool', lambda g_, p_=p_: g_.memset(xdtpad[p_], 0.0), w=['xdtpad%d' % p_])
            S.op('pool', lambda g_, p_=p_: g_.memset(XB[p_], 0.0), w=['XB%d' % p_])
        jsl = [slice(j * 128, (j + 1) * 128) for j in range(8)]

        def front(T):
            p = PAR[T]
            q = T % 2
            samp = (T == NTP)
            R = lambda n: '%s%d' % (n, p)
            hv = hTv_all[:, :, T * 128:(T + 1) * 128]
            hres = 'hT_%d' % T
            bA, nA = bank()
            for j in range(4):
                for c in range(8):
                    yield S.op('pe', mm(bA[:, jsl[j]], WMv[:, c, j * 128:(j + 1) * 128], hv[:, c, :], start=(c == 0), stop=(c == 7)),
                         r=[hres, 'WM%d' % c], w=[nA], inc=(c == 7 and j == 3))
            yield S.op('act', actf(sz[p], bA, AF.Silu), r=[nA], w=[R('sz')])
            release(nA)
            bB, nB = bank()
            for j in range(4):
                for c in range(8):
                    yield S.op('pe', mm(bB[:, jsl[j]], WMv[:, c, 512 + j * 128:512 + (j + 1) * 128], hv[:, c, :],
                                  start=(c == 0), stop=(c == 7)), r=[hres, 'WM%d' % c], w=[nB], inc=(c == 7 and j == 3))
            bC, nC = bank()
            for j in range(2):
                for c in range(8):
                    yield S.op('pe', mm(bC[:, jsl[j]], WMv[:, c, 1024 + j * 128:1024 + (j + 1) * 128], hv[:, c, :],
                                  start=(c == 0), stop=(c == 7)), r=[hres, 'WM%d' % c], w=[nC], inc=False)
            for c in range(8):
                yield S.op('pe', mm(bC[:, 256:264], hv[:, c, :], WMv[:, c, 1280 + 8 * g:1288 + 8 * g], start=(c == 0), stop=(c == 7)),
                     r=[hres, 'WM%d' % c], w=[nC], inc=(c == 7))
            smp = sm[p]
            xdtr = smp[:, 0:8]; dtt = smp[:, 8:16]; la = smp[:, 16:24]
            yield S.op('dve', tt(xdtr, bC[:, 256:264], dtb, ALU.add), r=[nC, 'mcp'], w=[R('sm_x')])
            if not samp:
                yield S.op('act', actf(Xp[p][:, 0:4, 3:131], v3(bB, 4), AF.Copy), r=[nB], w=[R('XB')])
                yield S.op('act', actf(Xp[p][:, 4:6, 3:131], v3(bC[:, 0:256], 2), AF.Copy), r=[nC], w=[R('XB')])
                release(nB); release(nC)
                yield S.op('dve', cp(tails[T % 3], Xp[p][:, :, 128:131]), r=[R('XB')], w=['tail%d' % (T % 3)])
                if T > 0:
                    yield S.op('dve', cp(Xp[p][:, :, 0:3], tails[(T - 1) % 3]), r=['tail%d' % ((T - 1) % 3)], w=[R('XB')])
                if T == NTP - 1:
                    for k in range(3):
                        yield S.dma('sp', bass.AP(cvp_h, k * 1536 + g * 512, [[1, 128], [128, 4]]), Xp[p][:, 0:4, 128 + k],
                              r=[R('XB')], w=['cvp_out%d_%d' % (g, next(uid))], allow_slow_non_contiguous=True)
                        for bi in (4, 5):
                            yield S.dma('sp', bass.AP(cvp_h, k * 1536 + choff[bi], [[1, 128], [1, 1]]),
                                  Xp[p][:, bi, 128 + k:129 + k], r=[R('XB')], w=['cvp_out%d_%d' % (g, next(uid))])
            else:
                yield S.op('pool', lambda g_: g_.memset(XB[p], 0.0), r=[R('XB')], w=[R('XB')])
                for bi, (co, cn) in enumerate([(g * 512, 512), (1024 + g * 128, 128), (1280 + g * 128, 128)]):
                    o_ = 0 if bi == 0 else (512 if bi == 1 else 640)
                    yield S.dma('sp', stcb[0:48, o_:o_ + cn], stc[:, co:co + cn], w=['stcb'])
                bS, nS = bank()
                for j in range(6):
                    yield S.op('pe', tr(bS[:, j * 48:(j + 1) * 48], stcb[0:48, jsl[j]], identf[0:48, 0:48]), r=['stcb', 'cst'],
                         w=[nS], inc=(j == 5))
                yield S.op('dve', cp(Xs[p][:, :, :, 0:3], bS[:, 0:288].rearrange('p (c b t) -> p c b t', c=6, b=16)),
                     r=[nS], w=[R('XB')])
                release(nS)
                yield S.op('act', actf(Xs[p][:, 0:4, :, 3:11], bB.rearrange('p (c b t) -> p c b t', c=4, b=16), AF.Copy),
                     r=[nB], w=[R('XB')])
                yield S.op('act', actf(Xs[p][:, 4:6, :, 3:11], bC[:, 0:256].rearrange('p (c b t) -> p c b t', c=2, b=16), AF.Copy),
                     r=[nC], w=[R('XB')])
                release(nB); release(nC)
                yield S.op('dve', cp(cvo[:, 0:288].rearrange('p (c b t) -> p c b t', c=6, b=16), Xs[p][:, :, :, 8:11]),
                     r=[R('XB')], w=['S0n0', 'S0n1'])
                bV, nV = bank()
                bV2, nV2 = bank()
                for j in range(6):
                    dst = bV[0:48, jsl[j]] if j < 4 else bV2[0:48, jsl[j - 4]]
                    yield S.op('pe', tr(dst, cvo[:, j * 48:(j + 1) * 48], identf), r=['S0n0', 'S0n1', 'cst'], w=[nV if j < 4 else nV2],
                         inc=(j in (3, 5)))
                yield S.op('act', actf(stcb[0:48, 0:512], bV[0:48, :], AF.Copy), r=[nV], w=['stcb'])
                yield S.op('act', actf(stcb[0:48, 512:768], bV2[0:48, 0:256], AF.Copy), r=[nV2], w=['stcb'])
                release(nV); release(nV2)
                for bi, (co, cn) in enumerate([(g * 512, 512), (1024 + g * 128, 128), (1280 + g * 128, 128)]):
                    o_ = 0 if bi == 0 else (512 if bi == 1 else 640)
                    yield S.dma('sp', cvs[:, co:co + cn], stcb[0:48, o_:o_ + cn], r=['stcb'], w=['cvs_out%d_%d' % (g, next(uid))])
            accv = acc[p].rearrange('p (c t) -> p c t', c=6)
            accs = acc[p].rearrange('p (c b t) -> p c b t', c=6, b=16)
            for j in range(6):
                if not samp:
                    o_ = accv[:, j, :]; xin = lambda k, j=j: Xp[p][:, j, k:k + 128]
                else:
                    o_ = accs[:, j, :, :]; xin = lambda k, j=j: Xs[p][:, j, :, k:k + 8]
                an = 'acc%d_%d' % (p, j)
                yield S.op('act', actf(o_, xin(0), AF.Identity, scale=cwv[:, j, 0:1], bias=cbt[:, j:j + 1]),
                     r=[R('XB'), 'mcp'], w=[an])
            for k in range(1, 4):
                for j in range(6):
                    if not samp:
                        o_ = accv[:, j, :]; xin = lambda k, j=j: Xp[p][:, j, k:k + 128]
                    else:
                        o_ = accs[:, j, :, :]; xin = lambda k, j=j: Xs[p][:, j, :, k:k + 8]
                    an = 'acc%d_%d' % (p, j)
                    yield S.op('dve', stt(o_, xin(k), cwv[:, j, k:k + 1], o_, ALU.mult, ALU.add), r=[R('XB'), 'mcp', an], w=[an])
            accn = ['acc%d_%d' % (p, j) for j in range(6)]
            yield S.op('act', actf(acc[p], acc[p], AF.Silu), r=accn, w=accn)
            yield S.op('act', actf(xab[p], acc[p], AF.Copy), r=accn, w=[R('xab')])
            yield S.op('act', actf(xdtr, xdtr, AF.Exp), r=[R('sm_x')], w=[R('sm_x')])
            yield S.op('act', actf(dtt, xdtr, AF.Ln, bias=1.0), r=[R('sm_x')], w=[R('sm_dt')])
            yield S.op('dve', tt(la, dtt, abc, ALU.mult), r=[R('sm_dt'), 'abc'], w=[R('sm_la')])

        def back(T):
            p = PAR[T]
            q = T % 2
            samp = (T == NTP)
            R = lambda n: '%s%d' % (n, p)
            c0 = C_TRIS if samp else C_TRIP
            tri = cst[:, c0:c0 + 128]
            c0 = C_STRS if samp else C_STRP
            strict = cst[:, c0:c0 + 128]
            smp = sm[p]
            dtt = smp[:, 8:16]; la = smp[:, 16:24]; cumt = smp[:, 24:32]; te = smp[:, 32:40]
            dtte = smp[:, 40:48]; cd = smp[:, 48:56]
            xabv = xab[p].rearrange('p (c t) -> p c t', c=6)
            accn = ['acc%d_%d' % (p, j) for j in range(6)]
            bE, nE = bank()
            yield S.op('pe', mm(bE[:, 0:8], tri, la), r=['cst', R('sm_la')], w=[nE], inc=False)
            yield S.op('pe', mm(bE[:, 8:16], strict, la), r=['cst', R('sm_la')], w=[nE], inc=False)
            yield S.op('pe', mm(bE[:, 16:24], onesf, la), r=['cst', R('sm_la')], w=[nE])
            yield S.op('act', actf(cumt, bE[:, 0:8], AF.Copy, scale=-1.0), r=[nE], w=[R('sm_cum')])
            yield S.op('act', actf(te, bE[:, 8:16], AF.Exp), r=[nE], w=[R('sm_te')])
            yield S.op('act', actf(cd, bE[:, 16:24], AF.Exp), r=[nE], w=[R('sm_cd')])
            release(nE)
            yield S.op('dve', tt(dtte, dtt, te, ALU.mult), r=[R('sm_dt'), R('sm_te')], w=[R('sm_dtte')])
            bG, nG = bank()
            bGb = bG.bitcast(BF)
            for j in range(4):
                yield S.op('pe', tr(bGb[:, jsl[j]], xabv[:, j, :], identb), r=[R('xab'), 'identb'], w=[nG], inc=False)
            yield S.op('pe', tr(bGb[:, 512:640], xabv[:, 4, :], identb), r=[R('xab'), 'identb'], w=[nG])
            bH, nH = bank()
            yield S.op('pe', mm(bH[:, 0:128], xabv[:, 4, :], xabv[:, 5, :]), r=[R('xab')], w=[nH])
            yield S.op('act', actf(Btok[p], bGb[:, 512:640], AF.Copy), r=[nG], w=[R('Btok')])
            yield S.op('dve', tt(cbm[p], bH[:, 0:128], tri, ALU.mult), r=[nH, 'cst'], w=[R('cbm')])
            release(nH)
            xdtpv = xdtpad[p].rearrange('p (j e c) -> p j e c', j=4, e=2)
            PTv = bGb[:, 0:512].rearrange('p (j e c) -> p j e c', j=4, e=2)
            dtv = dtt.rearrange('p (j e) -> p j e', e=2)
            for e in range(2):
                yield S.op('dve', tt(xdtpv[:, :, e, e * 64:(e + 1) * 64], PTv[:, :, e, :],
                               dtv[:, :, e:e + 1].to_broadcast([128, 4, 64]), ALU.mult), r=[nG, R('sm_dt')], w=[R('xdtpad')])
            yield S.op('dve', tt(xdtte[p].rearrange('p (h c) -> p h c', h=8), bGb[:, 0:512].rearrange('p (h c) -> p h c', h=8),
                           dtte.unsqueeze(2).to_broadcast([128, 8, 64]), ALU.mult), r=[nG, R('sm_dtte')], w=[R('xdtte')])
            release(nG)
            if not samp:
                bM, nM = bank()
                yield S.op('pe', mm(bM, Btok[p], xdtte[p]), r=[R('Btok'), R('xdtte')], w=[nM])
                STo = STt[q].rearrange('p (h c) -> p h c', h=8)
                STn = STt[1 - q]
                yield S.op('dve', tt(STn.rearrange('p (h c) -> p h c', h=8), STo, cd.unsqueeze(2).to_broadcast([128, 8, 64]), ALU.mult),
                           r=['STt%d' % q, R('sm_cd')], w=['STt%d' % (1 - q)])
                yield S.op('dve', tt(STn, STn, bM, ALU.add), r=['STt%d' % (1 - q), nM], w=['STt%d' % (1 - q)])
                release(nM)
                yield S.op('act', actf(STb[(T + 1) % 3], STn, AF.Copy), r=['STt%d' % (1 - q)], w=['STb%d' % ((T + 1) % 3)])
                if T == NTP - 1:
                    bW, nW = bank()
                    for j in range(4):
                        yield S.op('pe', tr(bW[:, jsl[j]], STn[:, jsl[j]], identf), r=['STt%d' % (1 - q), 'cst'], w=[nW], inc=(j == 3))
                    yield S.op('act', actf(Lh[p][:, 0:512], bW, AF.Copy), r=[nW], w=[R('Lh')])
                    release(nW)
                    yield S.dma('sp', ssp[g * 512:(g + 1) * 512].rearrange('(j p) n -> p j n', p=128), v3(Lh[p][:, 0:512], 4),
                          r=[R('Lh')], w=['ssp_out%d' % next(uid)])
            for half in range(2):
                bI, nI = bank()
                for hh in range(4):
                    h = half * 4 + hh
                    yield S.op('pe', mm(bI[:, jsl[hh]], la[:, h:h + 1].to_broadcast([128, 128]), tri), r=[R('sm_la'), 'cst'],
                               w=[nI], inc=(hh == 3))
                for hh in range(4):
                    h = half * 4 + hh
                    yield S.op('act', actf(Lh[p][:, jsl[h]], bI[:, jsl[hh]], AF.Exp, bias=cumt[:, h:h + 1]),
                               r=[nI, R('sm_cum')], w=([R('Lh'), 'Lhh%d_0' % p] if h == 0 else ['Lhh%d_%d' % (p, h)]))
                for e in range(2):
                    yield S.op('act', actf(fsT[p][e * 64:(e + 1) * 64, half * 256:(half + 1) * 256].rearrange('p (a t) -> p a t', a=2),
                                           bI[e * 64:(e + 1) * 64, :].rearrange('p (a e t) -> p a e t', a=2, e=2)[:, :, e, :],
                                           AF.Exp), r=[nI], w=[R('fsT')])
                release(nI)
            Lhn = ['Lhh%d_%d' % (p, h) for h in range(8)]
            yield S.op('dve', stt(v3(MT[p]), v3(Lh[p]), 1.0, cbm[p].unsqueeze(1).to_broadcast([128, 8, 128]), ALU.min, ALU.mult),
                       r=Lhn + [R('Lh'), R('cbm')], w=[R('MT')])
            bJ, nJ = bank()
            if not samp:
                for j in range(4):
                    yield S.op('pe', mm(bJ[:, jsl[j]], STb[T % 3][:, jsl[j]], xabv[:, 5, :]), r=['STb%d' % (T % 3), R('xab')], w=[nJ],
                         inc=(j == 3))
            else:
                yield S.dma('sp', v3(S0n[0], 4), sts[0][g * 512:(g + 1) * 512].rearrange('(j p) n -> p j n', p=128), w=['S0n0'])
                xrn = ['xdtb', 'stcb']
                yield S.op('dve', ts(xdtb[0], xdtte[p], rowm[:, 0:1], None, ALU.mult), r=[R('xdtte'), 'cst'], w=[xrn[0]])
                for b in range(16):
                    bs = b % 2
                    if b + 1 < 16:
                        yield S.dma('sp', v3(S0n[1 - bs], 4), sts[b + 1][g * 512:(g + 1) * 512].rearrange('(j p) n -> p j n', p=128),
                              w=['S0n%d' % (1 - bs)])
                    bT, nT = bank()
                    for j in range(4):
                        yield S.op('pe', tr(bT[:, jsl[j]], S0n[bs][:, jsl[j]], identf), r=['S0n%d' % bs, 'cst'], w=[nT], inc=(j == 3))
                    yield S.op('act', actf(S0T[bs], bT, AF.Copy), r=[nT], w=['S0T%d' % bs])
                    release(nT)
                    for j in range(4):
                        yield S.op('pe', mm(bJ[:, j * 128 + 8 * b:j * 128 + 8 * b + 8], S0T[bs][:, jsl[j]], xabv[:, 5, 8 * b:8 * b + 8]),
                             r=['S0T%d' % bs, R('xab')], w=[nJ], inc=(j == 3))
                    if b + 1 < 16:
                        yield S.op('dve', ts(xdtb[1 - bs], xdtte[p], rowm[:, b + 1:b + 2], None, ALU.mult), r=[R('xdtte'), 'cst'],
                                   w=[xrn[1 - bs]])
                    bQ, nQ = bank()
                    for j in range(4):
                        yield S.op('pe', mm(bQ[:, jsl[j]], xdtb[bs][:, jsl[j]], Btok[p]), r=[xrn[bs], R('Btok')], w=[nQ],
                             inc=(j == 3))
                    for j in range(4):
                        yield S.op('dve', stt(S0n[bs][:, jsl[j]], S0n[bs][:, jsl[j]], fsT[p][:, j * 128 + 8 * b + 7:j * 128 + 8 * b + 8],
                                        bQ[:, jsl[j]], ALU.mult, ALU.add), r=['S0n%d' % bs, R('fsT'), nQ], w=['S0n%d' % bs])
                    release(nQ)
                    yield S.dma('sp', sss[b][g * 512:(g + 1) * 512].rearrange('(j p) n -> p j n', p=128), v3(S0n[bs], 4),
                          r=['S0n%d' % bs], w=['sss_out%d' % next(uid)])
            yield S.op('dve', tt(tmp[p], bJ, fsT[p], ALU.mult), r=[nJ, R('fsT')], w=[R('tmp')])
            release(nJ)
            bK, nK = bank()
            for j in range(4):
                for e in range(2):
                    h = 2 * j + e
                    yield S.op('pe', mm(bK[:, jsl[j]], xdtpad[p][:, jsl[h]], MT[p][:, jsl[h]], start=(e == 0), stop=(e == 1)),
                         r=[R('xdtpad'), R('MT')], w=[nK], inc=(e == 1 and j == 3))
            yield S.op('dve', tt(tmp[p], tmp[p], bK, ALU.add), r=[R('tmp'), nK], w=[R('tmp')])
            release(nK)
            for j in range(4):
                yield S.op('dve', stt(tmp[p][:, jsl[j]], acc[p][:, jsl[j]], dcol[:, j:j + 1], tmp[p][:, jsl[j]], ALU.mult, ALU.add),
                     r=accn + ['mcp', R('tmp')], w=[R('tmp')])
            yield S.op('dve', tt(tmp[p], tmp[p], sz[p], ALU.mult), r=[R('tmp'), R('sz')], w=[R('tmp')])
            yield S.op('act', actf(ysq[p], tmp[p], AF.Square), r=[R('tmp')], w=[R('ysq')])
            bL, nL = bank()
            for j in range(4):
                yield S.op('pe', mm(bL[:, 0:128], onesb, ysq[p][:, jsl[j]], start=(j == 0), stop=(j == 3)), r=['onesb', R('ysq')],
                     w=[nL], inc=(j == 3))
            yield S.op('act', actf(rstdy[p], bL[:, 0:128], AF.Ln, scale=1.0 / 512, bias=epsc[:, 0:1]), r=[nL, 'epsc'], w=[R('rstdy')])
            release(nL)
            yield S.op('act', actf(rstdy[p], rstdy[p], AF.Exp, scale=-0.5), r=[R('rstdy')], w=[R('rstdy')])
            for j in range(4):
                yield S.op('dve', stt(mix2[p][:, jsl[j]], tmp[p][:, jsl[j]], mncol[:, g * 4 + j:g * 4 + j + 1], rstdy[p],
                                ALU.mult, ALU.mult), r=[R('tmp'), 'mncol', R('rstdy')], w=[R('mix2')])
            mixv = v3(mix2[p], 4)
            for n in range(2):
                bN, nN = bank()
                for c in range(4):
                    yield S.op('pe', mm(bN, mixv[:, c, :], WoMv[:, c, n * 512:(n + 1) * 512], start=(c == 0), stop=(c == 3)),
                         r=[R('mix2'), 'WoM'], w=[nN], inc=(c == 3))
                xs_ = x1t(T)[:, n * 512:(n + 1) * 512]
                yield S.op('dve', tt(xs_, bN, xs_, ALU.add), r=[nN, 'x1_%d' % T], w=['x1_%d' % T])
                release(nN)

        def chain(T):
            yield from front(T)
            if T == ORDER[-1]:
                prefetch_after('M%d' % g)
            yield from back(T)

        run_chains(chain, stagger=STAG_M)

    phase_M(0)
    if stop_after == 'M0':
        return dump_x1()
    phase_M(1)

    if stop_after == 'A2':
        for T in range(NT):
            dst = yp[T * 128:(T + 1) * 128, :] if T < NTP else ys
            S.dma('sp', dst, x1t(T), r=['x1_%d' % T], w=['y_out%d' % next(uid)])
        S.finish()
        S.emit()
        return nc

    S.barrier()
    ar.off = persist_mark
    h2v = hTv_all
    NQ = 8
    Wu = [ar.bf(8 * 512) for _ in range(2)]; Wd = [ar.bf(4 * 1024) for _ in range(2)]
    Wuv = [w.rearrange('p (c f) -> p c f', c=8) for w in Wu]
    Wdv = [w.rearrange('p (j d) -> p j d', j=4) for w in Wd]
    uT = [ar.bf(4 * 512) for _ in range(2)]
    sq = [ar.f32(512) for _ in range(2)]
    xn = ar.bf(1024); junk = ar.f32(1024)
    lnf_bc = ar.f32(1024)
    yo = [ar.f32(1024) for _ in range(2)]
    print('C arena words', ar.off, 'of', ar.n)
    S.dma('sp', lnf_bc, bc_ap(lnf_h, 1024), w=['lnf_bc'])

    def load_q(Q):
        s_ = Q % 2
        if not (Q < 2 and 'Wu' in prefetched):
            S.dma('pool', Wu[s_], wu_h.ap()[Q], w=['Wu%d' % s_])
        S.dma('pool', Wd[s_], wd_h.ap()[Q], w=['Wd%d' % s_])

    load_q(0)
    load_q(1)
    for T in range(NT):
        rms_to_hT(x1[:, T * D:(T + 1) * D], 'x1_%d' % T, xn, 'cxn', h2v[:, :, T * 128:(T + 1) * 128], 'h2T_%d' % T,
                  g2col, 'g2col', junk, 'cjunk')
    held.clear()
    PUP = [ps_t[:, i * 512:(i + 1) * 512] for i in range(4)]
    PDN = [ps_t[:, 2048:3072], ps_t[:, 3072:4096]]
    groups = [(0, 4), (4, 4), (8, 4), (12, 4), (16, 1)]
    seq = [(Q, gi) for Q in range(NQ) for gi in range(len(groups))]
    cnt = {'up': 0, 'dn': 0}

    def up(i):
        Q, gi = seq[i]
        t0, nt = groups[gi]
        s_ = Q % 2
        ntok = nt * 128
        us = i % 2
        uv = uT[us].rearrange('p (j t) -> p j t', j=4)
        hres = ['h2T_%d' % t for t in range(t0, t0 + nt)]
        for j in range(4):
            pb = cnt['up'] % 4; cnt['up'] += 1
            for c_ in range(8):
                S.op('pe', mm(PUP[pb][:, 0:ntok], Wuv[s_][:, c_, j * 128:(j + 1) * 128],
                              h2v[:, c_, t0 * 128:t0 * 128 + ntok], start=(c_ == 0), stop=(c_ == 7)),
                     r=hres + ['Wu%d' % s_], w=['PS%d' % pb], inc=(c_ == 7))
            S.op('act', actf(sq[j % 2][:, 0:ntok], PUP[pb][:, 0:ntok], AF.Square), r=['PS%d' % pb], w=['sq%d' % (j % 2)])
            S.op('dve', stt(uv[:, j, 0:ntok], PUP[pb][:, 0:ntok], 0.0, sq[j % 2][:, 0:ntok], ALU.is_gt, ALU.mult),
                 r=['PS%d' % pb, 'sq%d' % (j % 2)], w=['uT%d' % us])

    def down(i):
        Q, gi = seq[i]
        t0, nt = groups[gi]
        s_ = Q % 2
        us = i % 2
        uv = uT[us].rearrange('p (j t) -> p j t', j=4)
        for t in range(nt):
            T = t0 + t
            db = cnt['dn'] % 2; cnt['dn'] += 1
            for n in range(2):
                for j in range(4):
                    S.op('pe', mm(PDN[db][:, n * 512:(n + 1) * 512], uv[:, j, t * 128:(t + 1) * 128],
                                  Wdv[s_][:, j, n * 512:(n + 1) * 512], start=(j == 0), stop=(j == 3)),
                         r=['uT%d' % us, 'Wd%d' % s_], w=['PS%d' % (4 + 2 * db + n)], inc=(j == 3))
            xT = x1[:, T * D:(T + 1) * D]
            S.op('dve', tt(xT, PDN[db], xT, ALU.add), r=['PS%d' % (4 + 2 * db), 'PS%d' % (5 + 2 * db), 'x1_%d' % T],
                 w=['x1_%d' % T])
            if Q == NQ - 1:
                ys_ = T % 2
                S.op('dve', (lambda v, xT=xT: v.scalar_tensor_tensor(
                    out=junk, in0=xT, scalar=1.0, in1=xT, op0=ALU.mult, op1=ALU.mult, accum_out=rstd_s[:, 4:5])),
                    r=['x1_%d' % T], w=['cjunk', 'fssq'])
                S.op('act', actf(rstd_s[:, 5:6], rstd_s[:, 4:5], AF.Ln, scale=1.0 / D, bias=epsc[:, 0:1]),
                     r=['fssq', 'epsc'], w=['flnv'])
                S.op('act', actf(rstd_s[:, 6:7], rstd_s[:, 5:6], AF.Exp, scale=-0.5), r=['flnv'], w=['frstd'])
                S.op('dve', stt(yo[ys_], xT, rstd_s[:, 6:7], lnf_bc, ALU.mult, ALU.mult),
                     r=['x1_%d' % T, 'frstd', 'lnf_bc'], w=['yo%d' % ys_])
                dst = yp[T * 128:(T + 1) * 128, :] if T < NTP else ys
                S.dma('sp', dst, yo[ys_], r=['yo%d' % ys_], w=['y_out%d' % next(uid)])
        if gi == len(groups) - 1 and Q + 2 < NQ:
            load_q(Q + 2)

    up(0)
    for i in range(len(seq)):
        if i + 1 < len(seq):
            up(i + 1)
        down(i)
    S.finish()
    S.emit()
    return nc


def shard_inputs(inp):
    f = lambda a: np.ascontiguousarray(np.asarray(a, dtype=np.float32))
    cst = make_consts()
    w_in = np.asarray(inp['w_in'][0], np.float32)
    w_out = np.asarray(inp['w_out'][0], np.float32)
    w_up = np.asarray(inp['w_up'][0], np.float32)
    w_dn = np.asarray(inp['w_down'][0], np.float32)
    Wp = w_in.reshape(8, 128, DIN).transpose(1, 0, 2)
    w_h = np.stack([np.stack([Wp[:, :, k * 1024 + hb * 512:k * 1024 + (hb + 1) * 512] for k in range(4)], axis=2)
                    .reshape(128, 16384) for hb in range(2)])
    wo_h = np.stack([w_out[hb * 512:(hb + 1) * 512].reshape(4, 128, 1024).transpose(1, 0, 2).reshape(128, 4096)
                     for hb in range(2)])
    w_m = np.stack([np.concatenate([Wp[:, :, 4096 + g * 512:4096 + (g + 1) * 512], Wp[:, :, 5120 + g * 512:5120 + (g + 1) * 512],
                                    Wp[:, :, 6144 + g * 128:6144 + (g + 1) * 128], Wp[:, :, 6400 + g * 128:6400 + (g + 1) * 128],
                                    Wp[:, :, 6656:6672]], axis=2).reshape(128, 8 * 1296) for g in range(2)])
    wo_m = np.stack([w_out[1024 + g * 512:1024 + (g + 1) * 512].reshape(4, 128, 1024).transpose(1, 0, 2).reshape(128, 4096)
                     for g in range(2)])
    Up = w_up.reshape(8, 128, 4096).transpose(1, 0, 2)
    w_u = np.stack([Up[:, :, Q * 512:(Q + 1) * 512].reshape(128, 4096) for Q in range(8)])
    w_d = np.stack([w_dn[Q * 512:(Q + 1) * 512].reshape(4, 128, 1024).transpose(1, 0, 2).reshape(128, 4096) for Q in range(8)])
    cw = np.asarray(inp['conv_w'][0], np.float32); cb = np.asarray(inp['conv_b'][0], np.float32)
    dsk = np.asarray(inp['d_skip'][0], np.float32); dtb_ = np.asarray(inp['dt_bias'][0], np.float32)
    alog_ = np.asarray(inp['a_log'][0], np.float32)
    mcst = np.zeros((128, 100), np.float32)
    for g in range(2):
        choff = [g * 512 + j * 128 for j in range(4)] + [1024 + g * 128, 1280 + g * 128]
        o = g * 50
        for blk in range(6):
            for k in range(4):
                mcst[:, o + blk * 4 + k] = cw[k, choff[blk]:choff[blk] + 128]
            mcst[:, o + 24 + blk] = cb[choff[blk]:choff[blk] + 128]
        for j in range(4):
            for e in range(2):
                mcst[e * 64:(e + 1) * 64, o + 30 + j] = dsk[8 * g + 2 * j + e]
        mcst[:, o + 34:o + 42] = dtb_[8 * g:8 * g + 8][None, :]
        mcst[:, o + 42:o + 50] = alog_[8 * g:8 * g + 8][None, :]
    shared = {
        'lbl': f(inp['hg_lb_logits']), 'ln1': f(inp['ln1'][0]),
        'w_h': f(w_h), 'wo_h': f(wo_h), 'w_m': f(w_m), 'wo_m': f(wo_m), 'w_u': f(w_u), 'w_d': f(w_d),
        'hgn': f(inp['hg_norm'][0].reshape(1024)), 'mcst': mcst,
        'm_norm': f(inp['m_norm'][0]), 'ln2': f(inp['ln2'][0]), 'ln_f': f(inp['ln_f']), 'cst': cst,
    }
    maps = []
    for i in range(NCORES):
        m = dict(shared)
        m.update({
            'xp': f(inp['x_prompt'][i]),
            'xs': f(inp['x_sample'][16 * i:16 * i + 16].reshape(128, D)),
            'st_h': f(inp['state_hgrn'][0, 16 * i:16 * i + 16]),
            'st_s': f(inp['state_ssm'][0, 16 * i:16 * i + 16].reshape(16, 1024, 128)),
            'st_c': f(inp['state_conv'][0, 16 * i:16 * i + 16].reshape(48, 1536)),
        })
        maps.append(m)
    return maps


def gather(res):
    R = res.results
    y_prompt = np.stack([R[i]['yp'] for i in range(NCORES)]).astype(np.float32)
    y_sample = np.concatenate([R[i]['ys'].reshape(16, 8, D) for i in range(NCORES)]).astype(np.float32)
    hgp = np.stack([R[i]['hgp'] for i in range(NCORES)])[None].astype(np.float32)
    hgs = np.concatenate([R[i]['hgs'] for i in range(NCORES)])[None].astype(np.float32)
    ssp = np.stack([R[i]['ssp'].reshape(16, 64, 128) for i in range(NCORES)])[None].astype(np.float32)
    sss = np.concatenate([R[i]['sss'].reshape(16, 16, 64, 128) for i in range(NCORES)])[None].astype(np.float32)
    cvp = np.stack([R[i]['cvp'] for i in range(NCORES)])[None].astype(np.float32)
    cvs = np.concatenate([R[i]['cvs'].reshape(16, 3, 1536) for i in range(NCORES)])[None].astype(np.float32)
    return (y_prompt, y_sample, hgp, hgs, ssp, sss, cvp, cvs)


def kernel(**inputs):
    nc = build()
    maps = shard_inputs(inputs)
    res = run_bass_kernel_spmd(nc, maps, core_ids=list(range(NCORES)))
    return gather(res)
```
